# Optimizing a Trainium2 kernel written in Bass

```python
import jax
import jax.numpy as jnp
from jax import lax
import numpy as np

D_MODEL = 2048
BATCH = 4
SEQ = 2048
DEPTH = 2
DEC_BATCH = 128
DEC_SEQ = 1
PAST_LEN = 2048
PAGE_SIZE = 128

W_A = 2048
N_A_GROUPS = 8
A_GROUP = W_A // N_A_GROUPS
CHUNK = 128
W_B = 2048
N_B_BLOCKS = 16
B_BLOCK = W_B // N_B_BLOCKS
CONV_W = 4
C_RG = 8.0
N_HEADS = 16
HEAD_DIM = 128
N_KV = 4
N_IDX_HEADS = 16
IDX_DIM = 64
TOPK_MAX = 256
QBLOCK = 128
N_BRANCH = 3
EPS = 1e-6
SPLIT_SIZES = (W_A, W_A, W_A, W_B, W_B, N_HEADS * HEAD_DIM, N_KV * HEAD_DIM, N_KV * HEAD_DIM, N_IDX_HEADS * IDX_DIM, IDX_DIM, N_IDX_HEADS, N_HEADS * HEAD_DIM, N_BRANCH * D_MODEL)
N_IN = sum(SPLIT_SIZES)

kernel_name = 'hybrid_gmlp_rglru_dsa_decode_step'


def rmsnorm(x, g):
    x32 = x.astype(jnp.float32)
    y = x32 * lax.rsqrt(jnp.mean(x32 * x32, axis=-1, keepdims=True) + EPS)
    return (y * g.astype(jnp.float32)).astype(x.dtype)


def split_proj(p):
    points = [int(o) for o in np.cumsum(SPLIT_SIZES)[:-1]]
    return jnp.split(p, points, axis=-1)


def gmlp_spatial(u, v, w_s, b_s):
    B, T, _ = v.shape
    nc = -(-T // CHUNK)
    vp = jnp.pad(v, ((0, 0), (0, nc * CHUNK - T), (0, 0)))
    vc = vp.reshape(B, nc, CHUNK, N_A_GROUPS, A_GROUP)
    ws = w_s * jnp.tril(jnp.ones((CHUNK, CHUNK), w_s.dtype))
    s = jnp.einsum('gij,bcjgd->bcigd', ws, vc) + jnp.swapaxes(b_s, 0, 1)[None, None, :, :, None]
    return u * s.reshape(B, nc * CHUNK, W_A)[:, :T]


def causal_conv(x, prefix, w, b):
    T = x.shape[1]
    xp = jnp.concatenate([prefix.astype(x.dtype), x], axis=1)
    y = b
    for j in range(CONV_W):
        y = y + xp[:, j:j + T] * w[j]
    return y, xp[:, T:]


def rg_lru(xc, h0, w_a, b_a, w_x, b_x, lam):
    B, T, _ = xc.shape
    xb = xc.reshape(B, T, N_B_BLOCKS, B_BLOCK)
    r = jax.nn.sigmoid((jnp.einsum('btnc,ncd->btnd', xb, w_a).reshape(B, T, W_B) + b_a).astype(jnp.float32))
    i = jax.nn.sigmoid((jnp.einsum('btnc,ncd->btnd', xb, w_x).reshape(B, T, W_B) + b_x).astype(jnp.float32))
    log_a = C_RG * r * jax.nn.log_sigmoid(lam.astype(jnp.float32))
    a = jnp.exp(log_a)
    bt = jnp.sqrt(-jnp.expm1(2.0 * log_a)) * i * xc.astype(jnp.float32)

    def step(h, ab):
        a_t, b_t = ab
        h = a_t * h + b_t
        return h, h

    hT, hs = lax.scan(step, h0.astype(jnp.float32), (jnp.swapaxes(a, 0, 1), jnp.swapaxes(bt, 0, 1)))
    return jnp.swapaxes(hs, 0, 1).astype(xc.dtype), hT.astype(xc.dtype)


def dsa_core(q, q_idx, w_idx, k_idx_keys, q_pos, top_k, gather_kv):
    B, T = q.shape[:2]
    logits = jnp.einsum('bthd,bsd->bths', q_idx, k_idx_keys).astype(jnp.float32) * IDX_DIM ** -0.5
    score = jnp.einsum('bths,bth->bts', jax.nn.relu(logits), w_idx.astype(jnp.float32)) * N_IDX_HEADS ** -0.5
    key_pos = jnp.arange(k_idx_keys.shape[1], dtype=jnp.int32)
    score = jnp.where((key_pos[None, :] <= q_pos[:, None])[None], score, -jnp.inf)
    _, idx = lax.top_k(score, top_k)
    valid = idx <= q_pos[None, :, None]
    k_sel, v_sel = gather_kv(idx)
    qg = q.reshape(B, T, N_KV, N_HEADS // N_KV, HEAD_DIM)
    s = jnp.einsum('btngd,btsnd->btngs', qg, k_sel).astype(jnp.float32) * HEAD_DIM ** -0.5
    s = jnp.where(valid[:, :, None, None, :], s, -jnp.inf)
    p = jax.nn.softmax(s, axis=-1).astype(v_sel.dtype)
    o = jnp.einsum('btngs,btsnd->btngd', p, v_sel)
    return o.reshape(B, T, N_HEADS * HEAD_DIM)


def take_rows(a, idx):
    return jax.vmap(lambda ab, ib: ab[ib])(a, idx)


def attend_prompt(q, k, v, q_idx, k_idx, w_idx):
    B, T = q.shape[:2]
    nb = T // QBLOCK
    top_k = min(TOPK_MAX, T // 4)

    def gather_kv(idx):
        return take_rows(k, idx), take_rows(v, idx)

    def blocks(a):
        return jnp.swapaxes(a.reshape((B, nb, QBLOCK) + a.shape[2:]), 0, 1)

    pos = jnp.arange(T, dtype=jnp.int32).reshape(nb, QBLOCK)

    def one(args):
        qb, qib, wib, pb = args
        return dsa_core(qb, qib, wib, k_idx, pb, top_k, gather_kv)

    out = lax.map(one, (blocks(q), blocks(q_idx), blocks(w_idx), pos))
    return jnp.swapaxes(out, 0, 1).reshape(B, T, N_HEADS * HEAD_DIM)


def make_attend_sample(ck, cv, ckidx, page_table):
    def attend(q, k, v, q_idx, k_idx, w_idx):
        Bd, T = q.shape[:2]
        top_k = min(TOPK_MAX, (PAST_LEN + T) // 4)
        kidx_past = ckidx[page_table].reshape(Bd, PAST_LEN, IDX_DIM)
        keys = jnp.concatenate([kidx_past, k_idx.astype(kidx_past.dtype)], axis=1)
        pool_k = ck.reshape(-1, N_KV, HEAD_DIM)
        pool_v = cv.reshape(-1, N_KV, HEAD_DIM)

        def gather_kv(idx):
            is_past = (idx < PAST_LEN)[..., None, None]
            pc = jnp.minimum(idx, PAST_LEN - 1)
            phys = jnp.take_along_axis(page_table, (pc // PAGE_SIZE).reshape(Bd, -1), axis=1).reshape(idx.shape)
            row = phys * PAGE_SIZE + pc % PAGE_SIZE
            ni = jnp.clip(idx - PAST_LEN, 0, T - 1)
            k_sel = jnp.where(is_past, pool_k[row], take_rows(k, ni).astype(pool_k.dtype))
            v_sel = jnp.where(is_past, pool_v[row], take_rows(v, ni).astype(pool_v.dtype))
            return k_sel, v_sel

        q_pos = PAST_LEN + jnp.arange(T, dtype=jnp.int32)
        return dsa_core(q, q_idx, w_idx, keys, q_pos, top_k, gather_kv)

    return attend


def trunk_layer(x, c, prm, conv_prefix, h0, attend):
    B, T, _ = x.shape
    mod = jax.nn.silu(c) @ prm['w_mod'] + prm['b_mod']
    shift, scale, gate = jnp.split(mod[:, None, :], 3, axis=-1)
    h = rmsnorm(x, prm['g_pre']) * (1 + scale) + shift
    (u_a, v_a, z_a, x_b, z_b, q, k, v, q_i, k_i, w_i, z_c, g_m) = split_proj(h @ prm['w_in'])
    u_a = jax.nn.gelu(u_a)
    v_a = rmsnorm(jax.nn.gelu(v_a), prm['g_v'])
    o_a = gmlp_spatial(u_a, v_a, prm['w_s'], prm['b_s'])
    x_c, new_conv = causal_conv(x_b, conv_prefix, prm['w_conv'], prm['b_conv'])
    o_b, h_t = rg_lru(x_c, h0, prm['w_rg_a'], prm['b_rg_a'], prm['w_rg_x'], prm['b_rg_x'], prm['lam'])
    q = q.reshape(B, T, N_HEADS, HEAD_DIM)
    k = k.reshape(B, T, N_KV, HEAD_DIM)
    v = v.reshape(B, T, N_KV, HEAD_DIM)
    q_i = q_i.reshape(B, T, N_IDX_HEADS, IDX_DIM)
    o_c = attend(q, k, v, q_i, k_i, w_i)
    p_a = (o_a * jax.nn.silu(z_a)) @ prm['w_pa']
    p_b = (o_b * jax.nn.silu(z_b)) @ prm['w_pb']
    p_c = (o_c.astype(x.dtype) * jax.nn.silu(z_c)) @ prm['w_pc']
    g = jax.nn.sigmoid(g_m).reshape(B, T, N_BRANCH, D_MODEL)
    merged = g[:, :, 0] * p_a + g[:, :, 1] * p_b + g[:, :, 2] * p_c
    o = merged @ prm['w_out']
    x = x + gate * rmsnorm(o, prm['g_post'])
    return x, (k, v, k_i, new_conv, h_t, v_a)


def setup_inputs(seed: int = 0) -> dict:
    kit = iter(jax.random.split(jax.random.key(seed), 40))
    f32 = jnp.float32

    def nrm(shape, s=1.0):
        return jax.random.normal(next(kit), shape, f32) * s

    n_pages = PAST_LEN // PAGE_SIZE
    n_phys = (DEC_BATCH * n_pages * 5) // 4
    a_c = jax.random.uniform(next(kit), (DEPTH, W_B), f32, 0.9, 0.999)
    p = a_c ** (1.0 / C_RG)
    perm = jax.random.permutation(next(kit), n_phys)
    return {
        'x_prompt': nrm((BATCH, SEQ, D_MODEL)),
        'x_sample': nrm((DEC_BATCH, DEC_SEQ, D_MODEL)),
        'c_prompt': nrm((BATCH, D_MODEL)),
        'c_sample': nrm((DEC_BATCH, D_MODEL)),
        'cache_k': nrm((DEPTH, n_phys, PAGE_SIZE, N_KV, HEAD_DIM)),
        'cache_v': nrm((DEPTH, n_phys, PAGE_SIZE, N_KV, HEAD_DIM)),
        'cache_kidx': nrm((DEPTH, n_phys, PAGE_SIZE, IDX_DIM)),
        'state_conv': nrm((DEPTH, DEC_BATCH, CONV_W - 1, W_B)),
        'state_h': nrm((DEPTH, DEC_BATCH, W_B), 0.5),
        'page_table': perm[:DEC_BATCH * n_pages].reshape(DEC_BATCH, n_pages).astype(jnp.int32),
        'w_mod': nrm((DEPTH, D_MODEL, 3 * D_MODEL), 0.5 * D_MODEL ** -0.5),
        'b_mod': nrm((DEPTH, 3 * D_MODEL), 0.01),
        'g_pre': 1.0 + nrm((DEPTH, D_MODEL), 0.02),
        'w_in': nrm((DEPTH, D_MODEL, N_IN), D_MODEL ** -0.5),
        'g_v': 1.0 + nrm((DEPTH, W_A), 0.02),
        'w_s': nrm((DEPTH, N_A_GROUPS, CHUNK, CHUNK), CHUNK ** -0.5),
        'b_s': nrm((DEPTH, N_A_GROUPS, CHUNK), 0.01),
        'w_conv': nrm((DEPTH, CONV_W, W_B), CONV_W ** -0.5),
        'b_conv': nrm((DEPTH, W_B), 0.01),
        'w_rg_a': nrm((DEPTH, N_B_BLOCKS, B_BLOCK, B_BLOCK), B_BLOCK ** -0.5),
        'b_rg_a': nrm((DEPTH, W_B), 0.01),
        'w_rg_x': nrm((DEPTH, N_B_BLOCKS, B_BLOCK, B_BLOCK), B_BLOCK ** -0.5),
        'b_rg_x': nrm((DEPTH, W_B), 0.01),
        'lam': jnp.log(p) - jnp.log1p(-p),
        'w_pa': nrm((DEPTH, W_A, D_MODEL), W_A ** -0.5),
        'w_pb': nrm((DEPTH, W_B, D_MODEL), W_B ** -0.5),
        'w_pc': nrm((DEPTH, N_HEADS * HEAD_DIM, D_MODEL), (N_HEADS * HEAD_DIM) ** -0.5),
        'w_out': nrm((DEPTH, D_MODEL, D_MODEL), D_MODEL ** -0.5),
        'g_post': 1.0 + nrm((DEPTH, D_MODEL), 0.02),
    }


def reference(x_prompt, x_sample, c_prompt, c_sample, cache_k, cache_v, cache_kidx, state_conv, state_h, page_table, w_mod, b_mod, g_pre, w_in, g_v, w_s, b_s, w_conv, b_conv, w_rg_a, b_rg_a, w_rg_x, b_rg_x, lam, w_pa, w_pb, w_pc, w_out, g_post):
    y_p, y_s = x_prompt, x_sample
    bp = x_prompt.shape[0]
    kp, vp, kip, cvp, hp = [], [], [], [], []
    ks, vs, kis, cvs, hs, gvs = [], [], [], [], [], []
    for l in range(DEPTH):
        prm = {'w_mod': w_mod[l], 'b_mod': b_mod[l], 'g_pre': g_pre[l], 'w_in': w_in[l], 'g_v': g_v[l],
               'w_s': w_s[l], 'b_s': b_s[l], 'w_conv': w_conv[l], 'b_conv': b_conv[l],
               'w_rg_a': w_rg_a[l], 'b_rg_a': b_rg_a[l], 'w_rg_x': w_rg_x[l], 'b_rg_x': b_rg_x[l], 'lam': lam[l],
               'w_pa': w_pa[l], 'w_pb': w_pb[l], 'w_pc': w_pc[l], 'w_out': w_out[l], 'g_post': g_post[l]}
        zero_conv = jnp.zeros((bp, CONV_W - 1, W_B), x_prompt.dtype)
        zero_h = jnp.zeros((bp, W_B), x_prompt.dtype)
        y_p, st_p = trunk_layer(y_p, c_prompt, prm, zero_conv, zero_h, attend_prompt)
        attend_s = make_attend_sample(cache_k[l], cache_v[l], cache_kidx[l], page_table)
        y_s, st_s = trunk_layer(y_s, c_sample, prm, state_conv[l], state_h[l], attend_s)
        kp.append(st_p[0]); vp.append(st_p[1]); kip.append(st_p[2]); cvp.append(st_p[3]); hp.append(st_p[4])
        ks.append(st_s[0]); vs.append(st_s[1]); kis.append(st_s[2]); cvs.append(st_s[3]); hs.append(st_s[4]); gvs.append(st_s[5])
    return (y_p, y_s, jnp.stack(kp), jnp.stack(vp), jnp.stack(kip), jnp.stack(cvp), jnp.stack(hp), jnp.stack(ks), jnp.stack(vs), jnp.stack(kis), jnp.stack(cvs), jnp.stack(hs), jnp.stack(gvs))
```

```python
import contextlib
import os
import numpy as np
import concourse.bass as bass
import concourse.mybir as mybir
from concourse.bass_utils import run_bass_kernel_spmd

F32 = mybir.dt.float32
BF16 = mybir.dt.bfloat16
I32 = mybir.dt.int32
AF = mybir.ActivationFunctionType
ALU = mybir.AluOpType
AX = mybir.AxisListType
P = 128
NCORES = 8
EPS = 1e-6
C_RG = 8.0
NEG_CAUSAL = -1.0e30
NEG_TOPK = -3.0e30


class Cfg:
    def __init__(s, D=2048, T=2048, PAST=2048):
        s.D, s.T, s.PAST = D, T, PAST
        s.NS = 128
        s.NPG = PAST // 128
        s.NPHYS = (s.NS * s.NPG * 5) // 4
        s.WA = 2048
        s.WB = 2048
        s.NH, s.HD, s.NKV, s.NIH, s.ID = 16, 128, 4, 16, 64
        s.TT = 256
        s.KD = D // 128
        s.KB = s.WA // 128
        s.TOPK_P = min(256, T // 4)
        s.TOPK_S = min(256, (PAST + 1) // 4)
        sizes = (s.WA, s.WA, s.WA, s.WB, s.WB, s.NH * s.HD, s.NKV * s.HD, s.NKV * s.HD,
                 s.NIH * s.ID, s.ID, s.NIH, s.NH * s.HD, 3 * D)
        offs = [0]
        for z in sizes:
            offs.append(offs[-1] + z)
        s.NIN = offs[-1]
        (s.oU, s.oV, s.oZA, s.oXB, s.oZB, s.oQ, s.oK, s.oVV, s.oQI, s.oKI, s.oWI, s.oZC, s.oGM) = offs[:-1]
        dblk = [(c0, min(512, D - c0)) for c0 in range(0, D, 512)]
        win = []
        for base, width in ((s.oV, s.WA), (s.oU, s.WA), (s.oZA, s.WA), (s.oXB, s.WB), (s.oZB, s.WB), (s.oK, 512), (s.oVV, 512),
                            (s.oKI, 80), (s.oQI, 1024), (s.oQ, 2048), (s.oZC, 2048)):
            for c0 in range(0, width, 512):
                win.append((base + c0, min(512, width - c0)))
        for br in range(3):
            for c0, n in dblk:
                win.append((s.oGM + br * D + c0, n))
        s.BL = {"win": win, "wmod": [(w * D + c0, n) for w in range(3) for c0, n in dblk],
                "wpa": list(dblk), "wpb": list(dblk), "wpc": list(dblk), "wout": list(dblk)}
        assert sum(n for _, n in win) == s.NIN
        s.NBR = {k_: (len(v) + NCORES - 1) // NCORES for k_, v in s.BL.items()}
        s.BIDX = {k_: {c0: i for i, (c0, n) in enumerate(v)} for k_, v in s.BL.items()}


class Buf:
    __slots__ = ("name", "w", "r")

    def __init__(s, name):
        s.name, s.w, s.r = name, None, []


class Ker:
    ND = int(os.environ.get("KND", "24"))

    def __init__(s, nc, es):
        s.nc = nc
        s.E = {"pe": nc.tensor, "act": nc.scalar, "dve": nc.vector, "pool": nc.gpsimd, "sp": nc.sync}
        s.sem = {e: es.enter_context(nc.semaphore("sem_" + e)) for e in s.E}
        s.cnt = {e: 0 for e in s.E}
        s.dsem = [es.enter_context(nc.semaphore("dsem%d" % i)) for i in range(s.ND)]
        s.ndma = 0
        s.seen = {e: {} for e in s.E}
        s.strict = {"pool", "dve", "act"}
        s.es = es
        s.ncc = 0
        s.cctoks = []
        s.maxwait = {}

    def _wait(s, eng, tok):
        if tok is None:
            return
        kind, key, val = tok
        if kind == "e":
            if key == eng and eng not in s.strict:
                return
            sem = s.sem[key]
        else:
            sem = key
        sk = (kind, id(key) if kind != "e" else key)
        if s.seen[eng].get(sk, 0) >= val:
            return
        s.E[eng].wait_ge(sem, val)
        s.seen[eng][sk] = val
        s.maxwait[sk] = max(s.maxwait.get(sk, 0), val)

    def _deps(s, eng, reads, writes):
        for b in reads:
            s._wait(eng, b.w)
        for b in writes:
            s._wait(eng, b.w)
            for t in b.r:
                s._wait(eng, t)

    def _commit(s, tok, reads, writes):
        for b in reads:
            if tok[0] == "e":
                b.r = [t for t in b.r if not (t[0] == "e" and t[1] == tok[1])]
            b.r.append(tok)
        for b in writes:
            b.w = tok
            b.r = []

    def op(s, eng, fn, reads=(), writes=()):
        s._deps(eng, reads, writes)
        ins = fn()
        s.cnt[eng] += 1
        ins.then_inc(s.sem[eng], 1)
        s._commit(("e", eng, s.cnt[eng]), reads, writes)

    def dma(s, eng, out, in_, reads=(), writes=(), **kw):
        j = s.ndma
        s.ndma += 1
        i, v = j % s.ND, 16 * (j // s.ND + 1)
        if v > 16:
            s._wait(eng, ("d", s.dsem[i], v - 16))
        s._deps(eng, reads, writes)
        s.E[eng].dma_start(out=out, in_=in_, **kw).then_inc(s.dsem[i], 16)
        s._commit(("d", s.dsem[i], v), reads, writes)

    def collective(s, kind, ins_ap, outs_ap, reads, writes):
        s._deps("pool", reads, writes)
        sem = s.es.enter_context(s.nc.semaphore("ccsem%d" % s.ncc))
        s.ncc += 1
        s.nc.gpsimd.collective_compute(kind, ALU.bypass, ins=[ins_ap], outs=[outs_ap],
                                       replica_groups=[list(range(NCORES))]).then_inc(sem)
        s.cctoks.append(("c", sem, 1))
        s._commit(("c", sem, 1), reads, writes)

    def handoff(s, srcs, dsts):
        toks = []
        for b in srcs:
            if b.w is not None:
                toks.append(b.w)
            toks.extend(b.r)
        for d in dsts:
            d.r = d.r + toks

    def finish(s):
        for i in range(min(s.ndma, s.ND)):
            last = ((s.ndma - 1 - i) // s.ND) if s.ndma - 1 >= i else -1
            if last >= 0:
                s._wait("sp", ("d", s.dsem[i], 16 * (last + 1)))
        for t in s.cctoks:
            s._wait("sp", t)
        for (kind, key), val in s.maxwait.items():
            if kind == "e":
                assert val <= s.cnt[key], ("unreachable engine wait", key, val, s.cnt[key])
            elif kind == "d":
                idx = [i for i in range(s.ND) if id(s.dsem[i]) == key][0]
                uses = len([j for j in range(s.ndma) if j % s.ND == idx])
                assert val <= 16 * uses, ("unreachable dma wait", idx, val, uses)
        for e in s.E:
            if e != "sp" and s.cnt[e] > 0:
                s._wait("sp", ("e", e, s.cnt[e]))


def build(cfg):
    c = cfg
    nc = bass.Bass("TRN2", target_bir_lowering=False, dynamic_dma_scratch_size=int(os.environ.get("KSCR", "16384")))
    D, T, NS, KD, KB, TT = c.D, c.T, c.NS, c.KD, c.KB, c.TT
    WA, WB, NIN, NPG, PAST = c.WA, c.WB, c.NIN, c.NPG, c.PAST
    NSUB, NTT, NKB = TT // 128, T // TT, T // 128
    DC4 = (D + 511) // 512
    R8 = D // NCORES
    RB8 = WA // NCORES

    def din(name, shape, dt=F32):
        return nc.dram_tensor(name, list(shape), dt, kind="ExternalInput").ap()

    def dout(name, shape, dt=F32):
        return nc.dram_tensor(name, list(shape), dt, kind="ExternalOutput").ap()

    xp = din("xp", [T, D]); xs = din("xs", [NS, D])
    cp = din("cp", [P, D]); cs = din("cs", [NS, D])
    ck = din("ck", [c.NPHYS, 128, 128]); cv = din("cv", [c.NPHYS, 128, 128])
    cki = din("cki", [c.NPHYS, 128, c.ID])
    sconv = din("sconv", [2, NS, 3, WB]); sh = din("sh", [2, NS, WB])
    pt = din("pt", [NS, NPG], I32)
    sel = din("sel", [P, 4])
    wrows = {"wmod": D, "win": D, "wpa": WA, "wpb": WA, "wpc": WA, "wout": D}
    wsh = {nm: din(nm + "_s", [2, wrows[nm], c.NBR[nm] * 512]) for nm in ("wmod", "win", "wpa", "wpb", "wpc", "wout")}
    b_mod = din("b_mod", [2, 3 * D]); g_pre = din("g_pre", [2, D]); g_v = din("g_v", [2, WA])
    w_s = din("w_s", [2, 8, 128, 128]); b_s = din("b_s", [2, 8, 128])
    w_conv = din("w_conv", [2, 4, WB]); b_conv = din("b_conv", [2, WB])
    w_rg_a = din("w_rg_a", [2, 16, 128, 128]); b_rg_a = din("b_rg_a", [2, WB])
    w_rg_x = din("w_rg_x", [2, 16, 128, 128]); b_rg_x = din("b_rg_x", [2, WB])
    lam = din("lam", [2, WB]); g_post = din("g_post", [2, D])

    yp = dout("yp", [T, D]); ys = dout("ys", [NS, D])
    k_p = dout("k_p", [2, T, 512]); v_p = dout("v_p", [2, T, 512]); ki_p = dout("ki_p", [2, T, c.ID])
    conv_p = dout("conv_p", [2, 3, WB]); h_p = dout("h_p", [2, WB])
    k_s = dout("k_s", [2, NS, 512]); v_s = dout("v_s", [2, NS, 512]); ki_s = dout("ki_s", [2, NS, c.ID])
    conv_s = dout("conv_s", [2, NS, 3, WB]); h_s = dout("h_s", [2, NS, WB]); gv_s = dout("gv_s", [2, NS, WA])

    def dscr(name, shape, dt):
        return nc.dram_tensor(name, list(shape), dt)

    wspec = ["wmod", "win", "wpa", "wpb", "wpc", "wout"]
    WB_bounce, WFULL = {}, {}
    for nm in wspec:
        kk_ = wrows[nm] // 128
        for l in range(2):
            WB_bounce[nm, l] = dscr("bn_%s%d" % (nm, l), [c.NBR[nm], 128, kk_ * 512 + 64], BF16)
            WFULL[nm, l] = dscr("wf_%s%d" % (nm, l), [NCORES * c.NBR[nm], 128, kk_ * 512 + 64], BF16)
    x1p = dscr("x1p", [T, D], F32); x1s = dscr("x1s", [NS, D], F32)
    modD = {(l, g, w): dscr("mod_%d%d%d" % (l, g, w), [P, D], F32) for l in range(2) for g in range(2) for w in range(3)}
    oc_b = [dscr("oc_b%d" % l, [NS, 512], F32) for l in range(2)]
    oc_g = [dscr("oc_g%d" % l, [NCORES * NS, 512], F32) for l in range(2)]

    KDBG = int(os.environ.get("KDBG", "0"))
    dbg_t = dout("dbg", [32, P, 256]) if KDBG else None
    es = contextlib.ExitStack()
    with es:
        k = Ker(nc, es)

        def dbg(i, ap, rb):
            if KDBG:
                k.dma("pool", dbg_t[i][0:ap.shape[0], 0:ap.shape[1]], ap, reads=[rb], writes=[Buf("d")])
        bufs = {}

        def B(name):
            if name not in bufs:
                bufs[name] = Buf(name)
            return bufs[name]

        def sb(name, shape, dt):
            return nc.alloc_sbuf_tensor(name, list(shape), dt)

        NWB = 2
        wb = [sb("wb%d" % i, [P, 16, 512], BF16) for i in range(NWB)]
        wbB = [[B("wb%d_%d" % (i, q_)) for q_ in range(4)] for i in range(NWB)]
        hT = sb("hT", [P, KD, TT], BF16)
        brT = sb("brT", [P, 16, TT], BF16)
        mrg = sb("mrg", [P, KD, TT], F32)
        vn = sb("vn", [P, NSUB, WA], BF16)
        assert NSUB * WA * 2 >= T * 4
        sc = vn[:, :, :].rearrange("p a b -> p (a b)").bitcast(F32)[:, 0:T]
        X1t = sb("X1", [P, max(D, 2048)], F32)
        X2t = sb("X2", [P, max(D, 2048)], F32)
        X3t = sb("X3", [P, max(D, 2048)], BF16)
        xt = X1t[:, 0:D]; At = X2t[:, 0:D]; htok = X3t[:, 0:D]
        maskT = X1t[:, :].bitcast(BF16)[:, 0:NKB * NSUB * 128].rearrange("p (k s q) -> p k s q", s=NSUB, q=128)
        qiT = X2t[:, :].bitcast(BF16)[:, 0:8 * TT].rearrange("p (h t) -> p h t", t=TT)
        mk = X2t[:, :].bitcast(BF16)[:, 2048:2048 + T]
        szc = X3t[:, :].bitcast(F32)[:, 0:4 * TT].rearrange("p (g t) -> p g t", t=TT)
        gvb = sb("gvb", [P, WA], BF16)
        NTS, NTL = 12, 4
        tsm = sb("tsm", [P, NTS, 256], F32); tsmi = [0]
        tlg = sb("tlg", [P, NTL, 512], F32); tlgi = [0]

        def tmp_s(dt=F32, n=TT):
            i = tsmi[0] % NTS; tsmi[0] += 1
            ap = tsm[:, i, :]
            if dt == BF16:
                ap = ap.bitcast(BF16)
            return B("tsm%d" % i), ap[:, 0:n]

        def tmp_l(dt=F32, n=512):
            i = tlgi[0] % NTL; tlgi[0] += 1
            ap = tlg[:, i, :]
            if dt == BF16:
                ap = ap.bitcast(BF16)
            return B("tlg%d" % i), ap[:, 0:n]

        identb = sb("identb", [P, P], BF16); identf = sb("identf", [P, P], F32)
        cmask = sb("cmask", [P, P], F32); onesb = sb("onesb", [P, P], BF16)
        ones_row = sb("ones_row", [1, P], BF16)
        ones_f = sb("ones_f", [1, P], F32)
        WsT = sb("WsT", [P, 8, P], BF16); WsS = sb("WsS", [P, 8, P], BF16)
        bsrow = sb("bsrow", [1, 2, 8, P], BF16)
        wrga = sb("wrga", [P, 16, P], BF16); wrgx = sb("wrgx", [P, 16, P], BF16)
        NPB = 9
        pB = sb("pB", [P, 16, NPB], F32)
        halo = sb("halo", [P, 16, 3], F32); hc = sb("hc", [P, 16], F32)
        small = sb("small", [P, 64], F32)
        wabs = sb("wabs", [P, NSUB, 16], F32); wsg = sb("wsg", [P, NSUB, 16], F32)
        m8 = sb("m8", [P, 8], F32)
        selt = sb("selt", [P, 4], F32)
        ptt = sb("ptt", [P, NPG], I32)
        QBYTES = 38 * 1024
        Q = sb("Q", [P, QBYTES // 4], F32)
        Qb = Q[:, :].bitcast(BF16)
        o = 0
        KT = Qb[:, o:o + 4 * T].rearrange("p (n t) -> p n t", t=T); o += 4 * T
        Vres = Qb[:, o:o + NKB * 512].rearrange("p (k d) -> p k d", d=512); o += NKB * 512
        kiT = Qb[:, o:o + T]; o += T
        qT = Qb[:, o:o + 4 * TT].rearrange("p (g t) -> p g t", t=TT); o += 4 * TT
        assert o * 2 <= QBYTES
        MI = 8 * NPG
        NHK = max(1, (16 * NPG) // P)
        JP = NPG // NHK
        MK = 16 * JP
        NB = NHK * 8
        assert NB * MK == PAST and 16 * MI == PAST
        o = 0
        PK = PAST + 1
        PKp = ((PK + 1) // 2) * 2
        kgi = Q[:, o:o + 1024].rearrange("p (r d) -> p r d", d=64); o += 1024
        kTs = Q[:, o:o + PAST // 2].bitcast(BF16); o += PAST // 2
        rl = Q[:, o:o + PAST // 2].bitcast(BF16); o += PAST // 2
        scs = Q[:, o:o + PK]; o += PK + (PK % 2)
        o_idx_end = o
        mks = Q[:, o:o + PKp // 2].bitcast(BF16); o += PKp // 2
        maskTs = Q[:, o:o + NB * 64].bitcast(BF16).rearrange("p (j q) -> p j q", q=128); o += NB * 64
        vb = Q[:, o:o + (NB * 130) // 2].bitcast(BF16)[:, 0:NB * 130].rearrange("p (j d) -> p j d", d=130); o += (NB * 130) // 2
        mkp = Q[:, o:o + PAST // 2].bitcast(BF16).rearrange("p (b m) -> p b m", m=MK); o += PAST // 2
        assert o * 4 <= QBYTES, (o * 4, QBYTES)
        o = 0
        kgK = Q[:, o:o + NHK * 1024].rearrange("p (h r d) -> p h r d", r=8, d=128); o += NHK * 1024
        KTs = Q[:, o:o + PAST // 2].bitcast(BF16); o += PAST // 2
        assert o <= o_idx_end, (o, o_idx_end)
        vgV = vn[:, :, :].rearrange("p a b -> p (a b)").bitcast(F32)[:, 0:NHK * 1024].rearrange("p (h r d) -> p h r d", r=8, d=128)
        idxI = sb("idxI", [P, NS], I32); idxK = sb("idxK", [P, NHK, NS], I32)
        ptf = sb("ptf", [P, NPG], F32); SelI = sb("SelI", [16, P], F32); SelK = sb("SelK", [16, P], F32)
        ptm = sb("ptm", [16, NS], F32); jcol = sb("jcol", [16, 1], F32); mcol = sb("mcol", [P, 1], F32)
        jcol_i = sb("jcol_i", [16, 1], I32); mcol_i = sb("mcol_i", [P, 1], I32)
        qTn = sb("qTn", [P, 4, P], BF16); Pn = sb("Pn", [P, P, 4], BF16)
        vn1 = sb("vn1", [P, 130], BF16); kn_tok = sb("kn_tok", [P, P], F32)
        kiTs = sb("kiTs", [64, P], BF16); wTs = sb("wTs", [16, P], BF16)
        qiTs = X2t[0:64, :].bitcast(BF16)[:, 0:16 * P].rearrange("p (h t) -> p h t", t=P)
        qn_tok = sb("qn_tok", [P, 4, P], F32)
        osm = sb("osm", [4, 2, P], F32)

        psA = nc.alloc_psum_tensor("psA", [P, 4, 512], F32)
        psT = nc.alloc_psum_tensor("psT", [P, 2048], BF16)
        psO = nc.alloc_psum_tensor("psO", [P, 2, 512], F32)
        pbA = [B("psA%d" % i) for i in range(4)]
        bank_i = [0]

        def bank():
            i = bank_i[0] % 4
            bank_i[0] += 1
            return i

        PE, ACT, DVE, POOL, SP = nc.tensor, nc.scalar, nc.vector, nc.gpsimd, nc.sync

        wslot = [0]

        def wload(key, c0, ncols):
            Wt = WFULL[key]
            kk = (Wt.shape[2] - 64) // 512
            i = wslot[0] % NWB
            wslot[0] += 1
            bi_ = c.BIDX[key[0]][c0]
            assert c.BL[key[0]][bi_][1] == ncols
            src = Wt.ap()[(bi_ % NCORES) * c.NBR[key[0]] + bi_ // NCORES][:, 0:kk * 512].rearrange("p (kq cc) -> p kq cc", cc=512)
            for q_ in range(2):
                k0_, k1_ = q_ * 8, min(kk, q_ * 8 + 8)
                if k0_ < k1_:
                    k.dma("sp", wb[i][:, k0_:k1_, 0:ncols], src[:, k0_:k1_, 0:ncols], reads=[B("WF_%s%d" % key)], writes=[wbB[i][q_]])
            return i, kk

        def mm_fm(slot, kk, co, M, src, src_bufs, N):
            bi = bank()

            def f():
                for kc in range(kk):
                    ins = PE.matmul(out=psA[0:M, bi, 0:N], lhsT=wb[slot][:, kc, co:co + M], rhs=src(kc),
                                    start=(kc == 0), stop=(kc == kk - 1))
                return ins
            k.op("pe", f, reads=wbB[slot] + src_bufs, writes=[pbA[bi]])
            return bi

        def mm_tm(slot, kk, c0, ncols, src, src_bufs):
            bi = bank()

            def f():
                for kc in range(kk):
                    ins = PE.matmul(out=psA[:, bi, 0:ncols], lhsT=src(kc), rhs=wb[slot][:, kc, c0:c0 + ncols],
                                    start=(kc == 0), stop=(kc == kk - 1))
                return ins
            k.op("pe", f, reads=wbB[slot] + src_bufs, writes=[pbA[bi]])
            return bi

        def transposes_bf(src_ap_fn, n, src_bufs):
            def f():
                for j in range(n):
                    a_ = src_ap_fn(j)
                    ins = PE.transpose(out=psT[0:a_.shape[1], j * 128:(j + 1) * 128], in_=a_, identity=identb[:])
                return ins
            k.op("pe", f, reads=src_bufs + [B("identb")], writes=[B("psT")])

        def rstd_from(ss, n, rbufs):
            k.op("dve", lambda: DVE.tensor_scalar(out=ss, in0=ss, scalar1=1.0 / n, scalar2=EPS, op0=ALU.mult, op1=ALU.add),
                 reads=rbufs, writes=rbufs)
            k.op("act", lambda: ACT.activation(out=ss, in_=ss, func=AF.Sqrt), reads=rbufs, writes=rbufs)
            k.op("dve", lambda: DVE.reciprocal(out=ss, in_=ss), reads=rbufs, writes=rbufs)

        for l in range(2):
            for nm in wspec:
                wfB = B("WF_%s%d" % (nm, l))
                bnBs = []
                kk_ = wrows[nm] // 128
                for s_ in range(c.NBR[nm]):
                    srcv = wsh[nm][l][:, s_ * 512:(s_ + 1) * 512].rearrange("(kq p) cc -> p kq cc", p=128)
                    for q_ in range(0, kk_, 4):
                        q1_ = min(kk_, q_ + 4)
                        bb_ = Buf("bn")
                        bnBs.append(bb_)
                        k.dma("pool", WB_bounce[nm, l].ap()[s_][:, q_ * 512:q1_ * 512].rearrange("p (kq cc) -> p kq cc", cc=512), srcv[:, q_:q1_, :], writes=[bb_])
                k.collective("AllGather", WB_bounce[nm, l].ap().opt(), WFULL[nm, l].ap().opt(), reads=bnBs, writes=[wfB])

        if not os.environ.get("KNOBAR"):
            for e_ in ("sp", "pe", "act", "dve", "pool"):
                for t_ in k.cctoks:
                    k._wait(e_, t_)
        k.op("pool", lambda: POOL.memset(identf[:], 1.0), writes=[B("identf")])
        k.op("pool", lambda: POOL.affine_select(out=identf[:], in_=identf[:], pattern=[[-1, P]], compare_op=ALU.is_equal,
                                                fill=0.0, base=0, channel_multiplier=1),
             reads=[B("identf")], writes=[B("identf")])
        k.op("pool", lambda: POOL.tensor_copy(out=identb[:], in_=identf[:]), reads=[B("identf")], writes=[B("identb")])
        k.op("pool", lambda: POOL.memset(cmask[:], 0.0), writes=[B("cmask")])
        k.op("pool", lambda: POOL.affine_select(out=cmask[:], in_=cmask[:], pattern=[[-1, P]], compare_op=ALU.is_ge,
                                                fill=NEG_CAUSAL, base=0, channel_multiplier=1),
             reads=[B("cmask")], writes=[B("cmask")])
        k.op("pool", lambda: POOL.memset(onesb[:], 1.0), writes=[B("onesb")])
        k.op("pool", lambda: POOL.memset(ones_row[:], 1.0), writes=[B("ones_row")])
        k.op("pool", lambda: POOL.memset(ones_f[:], 1.0), writes=[B("ones_f")])
        k.dma("sp", selt[:], sel, writes=[B("selt")])
        k.dma("sp", ptt[:], pt, writes=[B("ptt")])

        X1b, X2b, X3b, SM = B("X1"), B("X2"), B("X3"), B("small")
        PB_, VNB = B("pB"), B("vn")

        def layer_setup(l):
            wtmp = X2t[:, 0:8 * P].rearrange("p (g j) -> p g j", j=P)
            k.dma("sp", wtmp, w_s[l].rearrange("g i j -> i g j"), writes=[X2b])
            k.op("pool", lambda: POOL.affine_select(out=wtmp, in_=wtmp, pattern=[[0, 8], [-1, P]], compare_op=ALU.is_ge,
                                                    fill=0.0, base=0, channel_multiplier=1), reads=[X2b], writes=[X2b])
            wtb = X3t[:, 0:8 * P].rearrange("p (g j) -> p g j", j=P)
            k.op("act", lambda: ACT.activation(out=wtb, in_=wtmp, func=AF.Copy), reads=[X2b], writes=[X3b])
            transposes_bf(lambda g: wtb[:, g, :], 8, [X3b])
            k.op("act", lambda: ACT.activation(out=WsT[:, :, :], in_=psT[:, 0:8 * P].rearrange("p (g i) -> p g i", i=P), func=AF.Copy),
                 reads=[B("psT")], writes=[B("WsT")])
            ws00 = small[:, 8:16]
            bw_ = bank()
            k.op("pe", lambda: PE.matmul(out=psA[:, bw_, 0:8], lhsT=ones_f[0:1, :], rhs=wtmp[0:1, :, 0], start=True, stop=True),
                 reads=[X2b, B("ones_f")], writes=[pbA[bw_]])
            act(ws00, psA[:, bw_, 0:8], AF.Copy, [pbA[bw_]], [SM])
            for g in range(8):
                k.op("dve", lambda g=g: DVE.tensor_scalar(out=WsS[:, g, :], in0=identf[:], scalar1=small[:, 8 + g:9 + g], scalar2=None, op0=ALU.mult),
                     reads=[SM, B("identf")], writes=[B("WsS")])
            k.dma("pool", bsrow[0:1, 0, :, :], b_s[l].rearrange("(o g) i -> o g i", o=1), writes=[B("bsrow")])
            k.op("dve", lambda: DVE.tensor_copy(out=bsrow[0:1, 1, :, :], in_=bsrow[0:1, 0, :, 0:1].to_broadcast([1, 8, P])),
                 reads=[B("bsrow")], writes=[B("bsrow")])
            vecs = [w_conv[l, 0], w_conv[l, 1], w_conv[l, 2], w_conv[l, 3], b_conv[l], b_rg_a[l], b_rg_x[l], lam[l]]
            pstg = X1t[0:16, 0:8 * P].rearrange("f (i p) -> f i p", p=P)
            for i, v in enumerate(vecs):
                k.dma("sp", pstg[:, i, :], v.rearrange("(f p) -> f p", p=P), writes=[X1b])
            bp_ = bank()

            def fpb():
                for i in range(8):
                    ins = PE.transpose(out=psA[:, bp_, i * 16:(i + 1) * 16], in_=pstg[:, i, :], identity=identf[0:16, 0:16])
                return ins
            k.op("pe", fpb, reads=[X1b, B("identf")], writes=[pbA[bp_]])
            act(pB[:, :, 0:8].rearrange("p f i -> p i f"), psA[:, bp_, 0:128].rearrange("p (i f) -> p i f", f=16), AF.Copy, [pbA[bp_]], [PB_])
            tl = small[:, 16:32]
            k.op("act", lambda: ACT.activation(out=tl, in_=pB[:, :, 7], func=AF.Exp, scale=-1.0), reads=[PB_], writes=[SM])
            k.op("act", lambda: ACT.activation(out=tl, in_=tl, func=AF.Ln, bias=1.0), reads=[SM], writes=[SM])
            k.op("dve", lambda: DVE.tensor_scalar(out=pB[:, :, 8], in0=tl, scalar1=-C_RG, scalar2=None, op0=ALU.mult), reads=[SM], writes=[PB_])
            for wsrc, wdst, nm in ((w_rg_a, wrga, "wrga"), (w_rg_x, wrgx, "wrgx")):
                wt2 = X1t[:, 0:16 * P].rearrange("p (n d) -> p n d", d=P)
                wtB = X1b
                k.dma("sp", wt2, wsrc[l].rearrange("n c d -> c n d"), writes=[wtB])
                k.op("act", lambda wdst=wdst, wt2=wt2: ACT.activation(out=wdst[:, :, :], in_=wt2, func=AF.Copy), reads=[wtB], writes=[B(nm)])
            gtmp = X2t[:, 0:WA]
            k.dma("sp", gtmp, g_v[l:l + 1, :].to_broadcast([P, WA]), writes=[X2b, B("mk")])
            k.op("act", lambda: ACT.activation(out=gvb[:], in_=gtmp, func=AF.Copy), reads=[X2b, B("mk")], writes=[B("gvb")])
            k.op("dve", lambda: DVE.memset(hc[:], 0.0), writes=[B("hc")])
            k.op("dve", lambda: DVE.memset(halo[:], 0.0), writes=[B("halo")])

        def mod_stage(l):
            vflat = vn[:, :, :].rearrange("p a b -> p (a b)")
            cT = [vflat[:, g * KD * P:(g + 1) * KD * P].rearrange("p (kk t) -> p kk t", t=P) for g in range(2)]
            for g, csrc in ((0, cp), (1, cs)):
                k.dma("sp", xt, csrc, writes=[X1b])
                k.op("act", lambda: ACT.activation(out=htok, in_=xt, func=AF.Silu), reads=[X1b], writes=[X3b])
                transposes_bf(lambda j: htok[:, j * P:(j + 1) * P], KD, [X3b])
                k.op("act", lambda g=g: ACT.activation(out=cT[g], in_=psT[:, 0:KD * P].rearrange("p (kk t) -> p kk t", t=P), func=AF.Copy),
                     reads=[B("psT")], writes=[VNB])
            ksub = int(os.environ.get("KSUB", "99"))
            if ksub < 1:
                return
            gpost_b = mrg[:, :, :].rearrange("p a b -> p (a b)")[:, 0:D]
            k.dma("sp", At, g_pre[l:l + 1, :].to_broadcast([P, D]), writes=[X2b, B("mk")])
            k.dma("sp", gpost_b, g_post[l:l + 1, :].to_broadcast([P, D]), writes=[B("mrg")])
            if ksub < 2:
                return
            for w in range(3):
                for cb in range(DC4):
                    c0 = cb * 512
                    ncols = min(512, D - c0)
                    slot, kk = wload(("wmod", l), w * D + c0, ncols)
                    if ksub < 3:
                        return
                    bB, bmb = tmp_l(F32, ncols)
                    k.dma("sp", bmb, b_mod[l:l + 1, w * D + c0:w * D + c0 + ncols].to_broadcast([P, ncols]), writes=[bB])
                    for g in range(2):
                        bi = mm_tm(slot, kk, 0, ncols, lambda kc, g=g: cT[g][:, kc, :], [VNB])
                        oB, o1 = tmp_l(F32, ncols)
                        pin = psA[:, bi, 0:ncols]
                        if w == 0:
                            k.op("dve", lambda: DVE.tensor_tensor(out=o1, in0=pin, in1=bmb, op=ALU.add), reads=[pbA[bi], bB], writes=[oB])
                            idx = 1
                        elif w == 1:
                            k.op("dve", lambda: DVE.tensor_tensor(out=o1, in0=pin, in1=bmb, op=ALU.add), reads=[pbA[bi], bB], writes=[oB])
                            k.op("dve", lambda: DVE.scalar_tensor_tensor(out=o1, in0=o1, scalar=1.0, in1=At[:, c0:c0 + ncols], op0=ALU.add, op1=ALU.mult),
                                 reads=[oB, X2b, B("mk")], writes=[oB])
                            idx = 0
                        else:
                            k.op("dve", lambda: DVE.tensor_tensor(out=o1, in0=pin, in1=bmb, op=ALU.add), reads=[pbA[bi], bB], writes=[oB])
                            k.op("dve", lambda: DVE.tensor_tensor(out=o1, in0=o1, in1=gpost_b[:, c0:c0 + ncols], op=ALU.mult),
                                 reads=[oB, B("mrg")], writes=[oB])
                            idx = 2
                        k.dma("pool", modD[l, g, idx].ap()[:, c0:c0 + ncols], o1, reads=[oB], writes=[B("modD%d%d%d" % (l, g, idx))])

        def pre(l, grp, xsrc, xsrcB, s):
            k.dma("sp", xt, xsrc, reads=[xsrcB], writes=[X1b])
            k.dma("sp", At, modD[l, grp, 0].ap(), reads=[B("modD%d%d%d" % (l, grp, 0))], writes=[X2b, B("mk")])
            ss = small[:, 0:1]
            k.op("act", lambda: ACT.activation(out=htok, in_=xt, func=AF.Square), reads=[X1b], writes=[X3b])
            k.op("dve", lambda: DVE.reduce_sum(out=ss, in_=htok, axis=AX.X), reads=[X3b], writes=[SM])
            rstd_from(ss, D, [SM])
            k.op("dve", lambda: DVE.scalar_tensor_tensor(out=At, in0=xt, scalar=ss, in1=At, op0=ALU.mult, op1=ALU.mult),
                 reads=[X1b, X2b, B("mk"), SM], writes=[X2b, B("mk")])
            k.dma("sp", xt, modD[l, grp, 1].ap(), reads=[B("modD%d%d%d" % (l, grp, 1))], writes=[X1b])
            k.op("pool", lambda: POOL.tensor_tensor(out=htok, in0=At, in1=xt, op=ALU.add), reads=[X1b, X2b, B("mk")], writes=[X3b])
            transposes_bf(lambda j: htok[:, j * P:(j + 1) * P], KD, [X3b])
            k.op("act", lambda: ACT.activation(out=hT[:, 0:KD, s * P:(s + 1) * P], in_=psT[:, 0:KD * P].rearrange("p (kk t) -> p kk t", t=P), func=AF.Copy),
                 reads=[B("psT")], writes=[B("hT")])

        HTB, BRB, MRB = B("hT"), B("brT"), B("mrg")

        def act(out, in_, func, reads, writes, **kw):
            k.op("act", lambda: ACT.activation(out=out, in_=in_, func=func, **kw), reads=reads, writes=writes)

        def tt(eng, out, in0, in1, op, reads, writes):
            e = DVE if eng == "dve" else POOL
            k.op(eng, lambda: e.tensor_tensor(out=out, in0=in0, in1=in1, op=op), reads=reads, writes=writes)

        def stt(out, in0, scalar, in1, op0, op1, reads, writes):
            k.op("dve", lambda: DVE.scalar_tensor_tensor(out=out, in0=in0, scalar=scalar, in1=in1, op0=op0, op1=op1),
                 reads=reads, writes=writes)

        def ts(out, in0, s1, s2, op0, op1, reads, writes):
            if s2 is None:
                k.op("dve", lambda: DVE.tensor_scalar(out=out, in0=in0, scalar1=s1, scalar2=None, op0=op0), reads=reads, writes=writes)
            else:
                k.op("dve", lambda: DVE.tensor_scalar(out=out, in0=in0, scalar1=s1, scalar2=s2, op0=op0, op1=op1), reads=reads, writes=writes)

        def stageA(l, grp, N):
            nsub = N // P
            key = ("win", l)
            for cb in range(WA // 512):
                slot, kk = wload(key, c.oV + cb * 512, 512)
                for s in range(nsub):
                    bi = mm_tm(slot, kk, 0, 512, lambda kc: hT[:, kc, s * P:(s + 1) * P], [HTB])
                    vsl = vn[:, s, cb * 512:(cb + 1) * 512]
                    act(vsl, psA[:, bi, :], AF.Gelu_apprx_tanh, [pbA[bi]], [VNB])
                    jB, junk = tmp_l(F32, 512)
                    act(junk, vsl, AF.Square, [VNB], [jB])
                    k.op("dve", lambda: DVE.reduce_sum(out=small[:, 32 + s * 4 + cb:33 + s * 4 + cb], in_=junk, axis=AX.X), reads=[jB], writes=[SM])
            for s in range(nsub):
                ss = small[:, 1:2]
                k.op("dve", lambda: DVE.reduce_sum(out=ss, in_=small[:, 32 + s * 4:36 + s * 4], axis=AX.X), reads=[SM], writes=[SM])
                rstd_from(ss, WA, [SM])
                if grp == 0:
                    stt(vn[:, s, :], vn[:, s, :], ss, gvb[:], ALU.mult, ALU.mult, [VNB, SM, B("gvb")], [VNB])
                else:
                    vf = X2t[:, 0:WA]
                    stt(vf, vn[:, 0, :], ss, gvb[:], ALU.mult, ALU.mult, [VNB, SM, B("gvb")], [X2b, B("mk")])
                    k.dma("pool", gv_s[l], vf, reads=[X2b, B("mk")], writes=[B("o_gv")])
                    k.op("pool", lambda: POOL.tensor_copy(out=vn[:, 0, :], in_=vf), reads=[X2b, B("mk")], writes=[VNB])
            Wsp = WsT if grp == 0 else WsS
            for g4 in range(KB // 4):
                slotU, kk = wload(key, c.oU + g4 * 512, 512)
                slotZ, _ = wload(key, c.oZA + g4 * 512, 512)
                for j in range(4):
                    fc = g4 * 4 + j
                    g = fc // 2
                    src = lambda kc: hT[:, kc, 0:N]
                    bu = mm_fm(slotU, kk, j * P, P, src, [HTB], N)
                    bz = mm_fm(slotZ, kk, j * P, P, src, [HTB], N)
                    bs = bank()

                    def f():
                        for cc in range(nsub):
                            PE.matmul(out=psA[:, bs, cc * P:(cc + 1) * P], lhsT=ones_row[0:1, :], rhs=bsrow[0:1, grp, g, :],
                                      start=True, stop=False)
                            ins = PE.matmul(out=psA[:, bs, cc * P:(cc + 1) * P], lhsT=vn[:, cc, fc * P:(fc + 1) * P], rhs=Wsp[:, g, :],
                                            start=False, stop=True)
                        return ins
                    k.op("pe", f, reads=[VNB, B("WsT"), B("WsS"), B("bsrow"), B("ones_row")], writes=[pbA[bs]])
                    aB, ta = tmp_s(F32, N); bB, tb = tmp_s(F32, N); cB, tc = tmp_s(F32, N)
                    act(ta, psA[:, bu, 0:N], AF.Gelu_apprx_tanh, [pbA[bu]], [aB])
                    act(tb, psA[:, bz, 0:N], AF.Silu, [pbA[bz]], [bB])
                    tt("dve", tc, ta, psA[:, bs, 0:N], ALU.mult, [aB, pbA[bs]], [cB])
                    tt("pool", brT[:, fc, 0:N], tc, tb, ALU.mult, [cB, bB], [BRB])

        def proj(l, br, wname, N):
            if KDBG and l == 0 and N == P:
                for ii, fc_ in enumerate((0, 5)):
                    fB_, f_ = tmp_s(F32, 256)
                    k.op("dve", lambda: DVE.tensor_copy(out=f_[:, 0:P], in_=brT[:, fc_, 0:P]), reads=[BRB], writes=[fB_])
                    dbg(24 + br * 2 + ii, f_[:, 0:P], fB_)
            for d4 in range(DC4):
                c0 = d4 * 512
                ncols = min(512, D - c0)
                slotP, kkP = wload((wname, l), c0, ncols)
                slotG, kkG = wload(("win", l), c.oGM + br * D + c0, ncols)
                for j in range(ncols // P):
                    dc = d4 * 4 + j
                    bp = mm_fm(slotP, kkP, j * P, P, lambda kc: brT[:, kc, 0:N], [BRB], N)
                    bg = mm_fm(slotG, kkG, j * P, P, lambda kc: hT[:, kc, 0:N], [HTB], N)
                    aB, ta = tmp_s(F32, N)
                    act(ta, psA[:, bg, 0:N], AF.Sigmoid, [pbA[bg]], [aB])
                    if br == 0:
                        tt("dve", mrg[:, dc, 0:N], ta, psA[:, bp, 0:N], ALU.mult, [aB, pbA[bp]], [MRB])
                    else:
                        cB, tc = tmp_s(F32, N)
                        tt("dve", tc, ta, psA[:, bp, 0:N], ALU.mult, [aB, pbA[bp]], [cB])
                        tt("pool", mrg[:, dc, 0:N], mrg[:, dc, 0:N], tc, ALU.add, [cB, MRB], [MRB])

        dbg_done = {}

        def stageB(l, grp, N, last_tile):
            key = ("win", l)
            HB, HCB = B("halo"), B("hc")
            for g4 in range(KB // 4):
                slotX, kk = wload(key, c.oXB + g4 * 512, 512)
                slotZ, _ = wload(key, c.oZB + g4 * 512, 512)
                for j in range(4):
                    fc = g4 * 4 + j
                    src = lambda kc: hT[:, kc, 0:N]
                    bx = mm_fm(slotX, kk, j * P, P, src, [HTB], N)
                    bz = mm_fm(slotZ, kk, j * P, P, src, [HTB], N)
                    szB, sz = tmp_s(F32, N)
                    act(sz, psA[:, bz, 0:N], AF.Silu, [pbA[bz]], [szB])
                    xcB, xc = tmp_s(F32, N)
                    w = lambda i: pB[:, fc, i:i + 1]
                    if grp == 0:
                        xbB, xfull = tmp_l(F32, 512)
                        xbt = xfull[:, 0:3 + N]
                        k.op("dve", lambda: DVE.tensor_copy(out=xbt[:, 0:3], in_=halo[:, fc, :]), reads=[HB], writes=[xbB])
                        act(xbt[:, 3:3 + N], psA[:, bx, 0:N], AF.Copy, [pbA[bx]], [xbB])
                        ts(xc, xbt[:, 0:N], w(0), w(4), ALU.mult, ALU.add, [xbB, PB_], [xcB])
                        for i in range(1, 4):
                            stt(xc, xbt[:, i:i + N], w(i), xc, ALU.mult, ALU.add, [xbB, PB_, xcB], [xcB])
                        k.op("dve", lambda: DVE.tensor_copy(out=halo[:, fc, :], in_=xbt[:, N:N + 3]), reads=[xbB], writes=[HB])
                    else:
                        stB, sfull = tmp_l(F32, 512)
                        stile = sfull.rearrange("p (j d) -> p j d", d=P)
                        k.dma("sp", stile[:, 0:3, :], sconv[l][:, :, fc * P:(fc + 1) * P], writes=[stB])
                        k.dma("sp", stile[:, 3, :], sh[l][:, fc * P:(fc + 1) * P], writes=[stB])
                        bt_ = bank()

                        def f():
                            for jj in range(4):
                                ins = PE.transpose(out=psA[:, bt_, jj * P:(jj + 1) * P], in_=stile[:, jj, :], identity=identf[:])
                            return ins
                        k.op("pe", f, reads=[stB, B("identf")], writes=[pbA[bt_]])
                        xbB, xbs = tmp_s(F32, N)
                        act(xbs, psA[:, bx, 0:N], AF.Copy, [pbA[bx]], [xbB])
                        ts(xc, psA[:, bt_, 0:P], w(0), w(4), ALU.mult, ALU.add, [pbA[bt_], PB_], [xcB])
                        stt(xc, psA[:, bt_, P:2 * P], w(1), xc, ALU.mult, ALU.add, [pbA[bt_], PB_, xcB], [xcB])
                        stt(xc, psA[:, bt_, 2 * P:3 * P], w(2), xc, ALU.mult, ALU.add, [pbA[bt_], PB_, xcB], [xcB])
                        stt(xc, xbs, w(3), xc, ALU.mult, ALU.add, [xbB, PB_, xcB], [xcB])
                    xcbB, xcb = tmp_s(BF16, N)
                    act(xcb, xc, AF.Copy, [xcB], [xcbB])
                    ba = bank()
                    k.op("pe", lambda: PE.matmul(out=psA[:, ba, 0:N], lhsT=wrga[:, fc, :], rhs=xcb, start=True, stop=True),
                         reads=[xcbB, B("wrga")], writes=[pbA[ba]])
                    bi_ = bank()
                    k.op("pe", lambda: PE.matmul(out=psA[:, bi_, 0:N], lhsT=wrgx[:, fc, :], rhs=xcb, start=True, stop=True),
                         reads=[xcbB, B("wrgx")], writes=[pbA[bi_]])
                    rB, r = tmp_s(F32, N); iB, ig = tmp_s(F32, N); aB, a = tmp_s(F32, N); eB, e2 = tmp_s(F32, N)
                    act(r, psA[:, ba, 0:N], AF.Sigmoid, [pbA[ba], PB_], [rB], bias=w(5))
                    act(ig, psA[:, bi_, 0:N], AF.Sigmoid, [pbA[bi_], PB_], [iB], bias=w(6))
                    act(a, r, AF.Exp, [rB, PB_], [aB], scale=w(8))
                    act(e2, a, AF.Square, [aB], [eB])
                    ts(e2, e2, -1.0, 1.0, ALU.mult, ALU.add, [eB], [eB])
                    ts(e2, e2, 1e-12, None, ALU.max, None, [eB], [eB])
                    act(e2, e2, AF.Sqrt, [eB], [eB])
                    tt("dve", ig, e2, ig, ALU.mult, [eB, iB], [iB])
                    tt("dve", ig, ig, xc, ALU.mult, [iB, xcB], [iB])
                    hB, hh = tmp_s(F32, N)
                    if KDBG and l == 0 and fc < 2 and not dbg_done.get((grp, fc)):
                        dbg_done[(grp, fc)] = 1
                        base = (grp * 2 + fc) * 6
                        dbg(base + 0, xc, xcB); dbg(base + 1, r, rB); dbg(base + 2, a, aB); dbg(base + 3, e2, eB); dbg(base + 4, ig, iB)
                    if grp == 0:
                        k.op("dve", lambda: DVE.tensor_tensor_scan(out=hh, data0=a, data1=ig, initial=hc[:, fc:fc + 1], op0=ALU.mult, op1=ALU.add),
                             reads=[aB, iB, HCB], writes=[hB])
                        k.op("dve", lambda: DVE.tensor_copy(out=hc[:, fc:fc + 1], in_=hh[:, N - 1:N]), reads=[hB], writes=[HCB])
                    else:
                        tt("dve", hh, a, psA[:, bt_, 3 * P:4 * P], ALU.mult, [aB, pbA[bt_]], [hB])
                        tt("dve", hh, hh, ig, ALU.add, [hB, iB], [hB])
                        bo = bank()

                        def f2():
                            PE.transpose(out=psA[:, bo, 0:P], in_=hh, identity=identf[:])
                            return PE.transpose(out=psA[:, bo, P:2 * P], in_=xbs, identity=identf[:])
                        k.op("pe", f2, reads=[hB, xbB, B("identf")], writes=[pbA[bo]])
                        oB, ot = tmp_s(F32, 256)
                        act(ot, psA[:, bo, 0:256], AF.Copy, [pbA[bo]], [oB])
                        k.dma("pool", h_s[l][:, fc * P:(fc + 1) * P], ot[:, 0:P], reads=[oB], writes=[B("o_hs")])
                        k.dma("pool", conv_s[l][:, 2, fc * P:(fc + 1) * P], ot[:, P:2 * P], reads=[oB], writes=[B("o_cs")])
                        k.dma("pool", conv_s[l][:, 0:2, fc * P:(fc + 1) * P], stile[:, 1:3, :], reads=[stB], writes=[B("o_cs")])
                    tt("pool", brT[:, fc, 0:N], hh, sz, ALU.mult, [hB, szB], [BRB])
            if grp == 0 and last_tile:
                bo_ = bank()

                def fco():
                    for jj in range(3):
                        PE.transpose(out=psA[0:16, bo_, jj * P:(jj + 1) * P], in_=halo[:, :, jj], identity=identf[:])
                    return PE.transpose(out=psA[0:16, bo_, 3 * P:4 * P], in_=hc[:, :], identity=identf[:])
                k.op("pe", fco, reads=[HB, HCB, B("identf")], writes=[pbA[bo_]])
                coB, cot = tmp_l(F32, 512)
                act(cot[0:16, :], psA[0:16, bo_, :], AF.Copy, [pbA[bo_]], [coB])
                for jj in range(3):
                    k.dma("pool", conv_p[l, jj].rearrange("(f p) -> f p", p=P), cot[0:16, jj * P:(jj + 1) * P], reads=[coB], writes=[OUTB()])
                k.dma("pool", h_p[l].rearrange("(f p) -> f p", p=P), cot[0:16, 3 * P:4 * P], reads=[coB], writes=[OUTB()])

        KTB, VRB, KIB, QTB, MKB, PSOB, WQB = B("KT"), B("Vres"), B("kiT"), B("qT"), B("mk"), B("psO"), B("wq")

        def OUTB():
            return Buf("o")

        def stageC_prompt(l, ti):
            N = TT
            key = ("win", l)
            tok0 = ti * TT
            hsrc = lambda kc: hT[:, kc, 0:N]
            slotK, kk = wload(key, c.oK, 512)
            for s in range(NSUB):
                r0 = tok0 + s * P
                bi = mm_tm(slotK, kk, 0, 512, lambda kc: hT[:, kc, s * P:(s + 1) * P], [HTB])
                oB, ot = tmp_l(F32, 512)
                act(ot, psA[:, bi, :], AF.Copy, [pbA[bi]], [oB])
                k.dma("pool", k_p[l][r0:r0 + P, :], ot, reads=[oB], writes=[OUTB()])
            for n in range(4):
                bi = mm_fm(slotK, kk, n * P, P, hsrc, [HTB], N)
                act(KT[:, n, tok0:tok0 + N], psA[:, bi, 0:N], AF.Copy, [pbA[bi]], [KTB])
            slotV, kk = wload(key, c.oVV, 512)
            for s in range(NSUB):
                r0 = tok0 + s * P
                kb = ti * NSUB + s
                bi = mm_tm(slotV, kk, 0, 512, lambda kc: hT[:, kc, s * P:(s + 1) * P], [HTB])
                oB, ot = tmp_l(F32, 512)
                act(ot, psA[:, bi, :], AF.Copy, [pbA[bi]], [oB])
                k.dma("pool", v_p[l][r0:r0 + P, :], ot, reads=[oB], writes=[OUTB()])
                k.op("pool", lambda: POOL.tensor_copy(out=Vres[:, kb, :], in_=ot), reads=[oB], writes=[VRB])
            slotI, kk = wload(key, c.oKI, 80)
            bik_ = c.BIDX["win"][c.oKI]
            ksrc = WFULL[key].ap()[(bik_ % NCORES) * c.NBR["win"] + bik_ // NCORES][:, 0:kk * 512].rearrange("p (kq cc) -> p kq cc", cc=512)[:, :, 0:64]
            for q_ in range(2):
                k0_, k1_ = q_ * 8, min(kk, q_ * 8 + 8)
                if k0_ < k1_:
                    k.dma("sp", wb[slotI][:, k0_:k1_, 128:192], ksrc[:, k0_:k1_, :], reads=[B("WF_%s%d" % key)], writes=[wbB[slotI][q_]])
                    k.dma("sp", wb[slotI][:, k0_:k1_, 192:256], ksrc[:, k0_:k1_, :], reads=[B("WF_%s%d" % key)], writes=[wbB[slotI][q_]])
            for s in range(NSUB):
                r0 = tok0 + s * P
                bi = mm_tm(slotI, kk, 0, 80, lambda kc: hT[:, kc, s * P:(s + 1) * P], [HTB])
                oB, ot = tmp_s(F32, 80)
                act(ot, psA[:, bi, 0:80], AF.Copy, [pbA[bi]], [oB])
                k.dma("pool", ki_p[l][r0:r0 + P, :], ot[:, 0:64], reads=[oB], writes=[OUTB()])
                act(wabs[:, s, :], ot[:, 64:80], AF.Abs, [oB], [WQB])
                act(wsg[:, s, :], ot[:, 64:80], AF.Sign, [oB], [WQB])
            bi = mm_fm(slotI, kk, 128, P, hsrc, [HTB], N)
            act(kiT[:, tok0:tok0 + N], psA[:, bi, 0:N], AF.Copy, [pbA[bi]], [KIB])
            for q4 in range(2):
                slotQ, kk = wload(key, c.oQI + q4 * 512, 512)
                for j in range(4):
                    pr = q4 * 4 + j
                    bi = mm_fm(slotQ, kk, j * P, P, hsrc, [HTB], N)
                    act(qiT[:, pr, 0:N], psA[:, bi, 0:N], AF.Copy, [pbA[bi]], [X2b])
            for s in range(NSUB):
                qt = ti * NSUB + s
                L = (qt + 1) * P
                for h in range(16):
                    pr, plo = h // 2, (h % 2) * 64
                    for c5 in range((L + 511) // 512):
                        k0 = c5 * 512
                        wdt = min(512, L - k0)
                        bi = bank()
                        k.op("pe", lambda: PE.matmul(out=psA[:, bi, 0:wdt], lhsT=qiT[plo:plo + 64, pr, s * P:(s + 1) * P],
                                                     rhs=kiT[plo:plo + 64, k0:k0 + wdt], start=True, stop=True),
                             reads=[X2b, KIB], writes=[pbA[bi]])
                        rB, tr = tmp_l(F32, 512)
                        act(tr[:, 0:wdt], psA[:, bi, 0:wdt], AF.Relu, [pbA[bi], WQB], [rB], scale=wabs[:, s, h:h + 1])
                        if h == 0:
                            ts(sc[:, k0:k0 + wdt], tr[:, 0:wdt], wsg[:, s, 0:1], None, ALU.mult, None, [rB, WQB], [VNB])
                        else:
                            stt(sc[:, k0:k0 + wdt], tr[:, 0:wdt], wsg[:, s, h:h + 1], sc[:, k0:k0 + wdt], ALU.mult, ALU.add,
                                [rB, WQB, VNB], [VNB])
                tt("dve", sc[:, L - P:L], sc[:, L - P:L], cmask[:], ALU.add, [VNB, B("cmask")], [VNB])
                if L > c.TOPK_P:
                    for it in range(c.TOPK_P // 8):
                        k.op("dve", lambda: DVE.max(out=m8[:], in_=sc[:, 0:L]), reads=[VNB], writes=[B("m8")])
                        k.op("dve", lambda: DVE.match_replace(out=sc[:, 0:L], in_to_replace=m8[:], in_values=sc[:, 0:L], imm_value=NEG_TOPK),
                             reads=[VNB, B("m8")], writes=[VNB])
                    ts(mk[:, 0:L], sc[:, 0:L], -2.0e30, None, ALU.is_le, None, [VNB], [MKB])
                else:
                    ts(mk[:, 0:L], sc[:, 0:L], -1.0e29, None, ALU.is_ge, None, [VNB], [MKB])
                transposes_bf(lambda j: mk[:, j * P:(j + 1) * P], qt + 1, [MKB])
                act(maskT[:, 0:qt + 1, s, :], psT[:, 0:(qt + 1) * P].rearrange("p (kk q) -> p kk q", q=P), AF.Copy, [B("psT")], [X1b])
            for n in range(4):
                slotQ, kk = wload(key, c.oQ + n * 512, 512)
                slotZ, _ = wload(key, c.oZC + n * 512, 512)
                for g in range(4):
                    bi = mm_fm(slotQ, kk, g * P, P, hsrc, [HTB], N)
                    act(qT[:, g, 0:N], psA[:, bi, 0:N], AF.Copy, [pbA[bi]], [QTB])
                    bi = mm_fm(slotZ, kk, g * P, P, hsrc, [HTB], N)
                    act(szc[:, g, 0:N], psA[:, bi, 0:N], AF.Silu, [pbA[bi]], [X3b])
                for s in range(NSUB):
                    qt = ti * NSUB + s
                    for kb in range(qt + 1):
                        bS = bank()
                        k.op("pe", lambda: PE.matmul(out=psA[:, bS, :], lhsT=KT[:, n, kb * P:(kb + 1) * P], rhs=qT[:, :, s * P:(s + 1) * P],
                                                     start=True, stop=True), reads=[KTB, QTB], writes=[pbA[bS]])
                        eB, eS = tmp_l(BF16, 512)
                        act(eS, psA[:, bS, :], AF.Exp, [pbA[bS]], [eB], scale=float(c.HD) ** -0.5)
                        ptB, pT = tmp_l(BF16, 512)
                        tt("pool", pT.rearrange("p (g q) -> p g q", q=P), eS.rearrange("p (g q) -> p g q", q=P),
                           maskT[:, kb, s, :].unsqueeze(1).to_broadcast([P, 4, P]), ALU.mult, [eB, X1b], [ptB])

                        def f():
                            PE.matmul(out=psO[:, 0, :], lhsT=Vres[:, kb, n * P:(n + 1) * P], rhs=pT, start=(kb == 0), stop=(kb == qt))
                            return PE.matmul(out=psO[:, 1, :], lhsT=onesb[:], rhs=pT, start=(kb == 0), stop=(kb == qt))
                        k.op("pe", f, reads=[VRB, ptB, B("onesb")], writes=[PSOB])
                    rB, rinv = tmp_l(F32, 512)
                    k.op("dve", lambda: DVE.reciprocal(out=rinv, in_=psO[:, 1, :]), reads=[PSOB], writes=[rB])
                    oB, oc = tmp_l(F32, 512)
                    tt("dve", oc, psO[:, 0, :], rinv, ALU.mult, [PSOB, rB], [oB])
                    tt("pool", brT[:, 4 * n:4 * n + 4, s * P:(s + 1) * P], oc.rearrange("p (g q) -> p g q", q=P),
                       szc[:, :, s * P:(s + 1) * P], ALU.mult, [oB, X3b], [BRB])

        def out_post(l, grp, N, xsrc_fn, xsrcB_fn, xdst_fn, xdstB_fn):
            nsub = N // P
            act(brT[:, 0:KD, 0:N], mrg[:, :, 0:N], AF.Copy, [MRB], [BRB])
            for s in range(nsub):
                banks = []
                for cb in range(DC4):
                    c0 = cb * 512
                    ncols = min(512, D - c0)
                    slot, kk = wload(("wout", l), c0, ncols)
                    bi = mm_tm(slot, kk, 0, ncols, lambda kc: brT[:, kc, s * P:(s + 1) * P], [BRB])
                    banks.append((bi, c0, ncols))
                    jB, junk = tmp_l(F32, 512)
                    act(junk[:, 0:ncols], psA[:, bi, 0:ncols], AF.Square, [pbA[bi]], [jB])
                    k.op("dve", lambda: DVE.reduce_sum(out=small[:, 40 + cb:41 + cb], in_=junk[:, 0:ncols], axis=AX.X), reads=[jB], writes=[SM])
                ss = small[:, 2:3]
                k.op("dve", lambda: DVE.reduce_sum(out=ss, in_=small[:, 40:40 + DC4], axis=AX.X), reads=[SM], writes=[SM])
                rstd_from(ss, D, [SM])
                k.dma("sp", xt, xsrc_fn(s), reads=[xsrcB_fn(s)], writes=[X1b])
                k.dma("sp", At, modD[l, grp, 2].ap(), reads=[B("modD%d%d%d" % (l, grp, 2))], writes=[X2b, MKB])
                for (bi, c0, ncols) in banks:
                    stt(At[:, c0:c0 + ncols], psA[:, bi, 0:ncols], ss, At[:, c0:c0 + ncols], ALU.mult, ALU.mult,
                        [pbA[bi], SM, X2b, MKB], [X2b, MKB])
                tt("pool", xt, xt, At, ALU.add, [X1b, X2b, MKB], [X1b])
                k.dma("pool", xdst_fn(s), xt, reads=[X1b], writes=[xdstB_fn(s)])

        iota_i = sb("iota_i", [16, P], I32)
        iota_f = sb("iota_f", [16, P], F32)
        wTf = sb("wTf", [16, P], F32)
        Wsel = sb("Wsel", [16, 2, P], BF16)
        k.op("pool", lambda: POOL.iota(out=iota_i[:], pattern=[[1, P]], base=0, channel_multiplier=0), writes=[B("iota")])
        k.op("pool", lambda: POOL.tensor_copy(out=iota_f[:], in_=iota_i[:]), reads=[B("iota")], writes=[B("iota")])
        psS = [psO[:, 0, :], psO[:, 1, :], psT[:, 0:1024].bitcast(F32), psT[:, 1024:2048].bitcast(F32)]
        psSB = [PSOB, PSOB, B("psT"), B("psT")]
        IXB = B("idx")
        k.op("pool", lambda: POOL.iota(out=jcol_i[:], pattern=[[0, 1]], base=0, channel_multiplier=-1), writes=[IXB])
        k.op("pool", lambda: POOL.tensor_copy(out=jcol[:], in_=jcol_i[:]), reads=[IXB], writes=[IXB])
        k.op("pool", lambda: POOL.iota(out=mcol_i[:], pattern=[[0, 1]], base=0, channel_multiplier=1), writes=[IXB])
        k.op("pool", lambda: POOL.tensor_copy(out=mcol[:], in_=mcol_i[:]), reads=[IXB], writes=[IXB])
        k.op("dve", lambda: DVE.tensor_copy(out=ptf[:], in_=ptt[:]), reads=[B("ptt")], writes=[IXB])
        for Sel, grp_sz in ((SelI, 8), (SelK, 16)):
            k.op("pool", lambda: POOL.memset(Sel[:], 1.0), reads=[IXB], writes=[IXB])
            k.op("pool", lambda: POOL.affine_select(out=Sel[:], in_=Sel[:], pattern=[[1, P]], compare_op=ALU.is_ge, fill=0.0,
                                                    base=0, channel_multiplier=-grp_sz), reads=[IXB], writes=[IXB])
            k.op("pool", lambda: POOL.affine_select(out=Sel[:], in_=Sel[:], pattern=[[-1, P]], compare_op=ALU.is_ge, fill=0.0,
                                                    base=grp_sz - 1, channel_multiplier=grp_sz), reads=[IXB], writes=[IXB])

        def make_idx(col0, npages, Sel, mult, M, dst):
            bi = bank()
            k.op("pe", lambda: PE.transpose(out=psA[0:npages, bi, 0:P], in_=ptf[:, col0:col0 + npages], identity=identf[:]),
                 reads=[IXB, B("identf")], writes=[pbA[bi]])
            ts(ptm[0:npages, :], psA[0:npages, bi, 0:NS], jcol[0:npages, 0:1], None, ALU.add, None, [pbA[bi], IXB], [IXB])
            b2 = bank()
            k.op("pe", lambda: PE.matmul(out=psA[0:M, b2, 0:NS], lhsT=Sel[0:npages, 0:M], rhs=ptm[0:npages, :], start=True, stop=True),
                 reads=[IXB], writes=[pbA[b2]])
            ts(dst, psA[0:M, b2, 0:NS], float(mult), mcol[0:M, 0:1], ALU.mult, ALU.add, [pbA[b2], IXB], [IXB])

        make_idx(0, NPG, SelI, 8, MI, idxI[0:MI, :])
        for h in range(NHK):
            make_idx(JP * h, JP, SelK, 16, MK, idxK[0:MK, h, :])
        cki_v = cki.rearrange("n (g r) d -> (n g) (r d)", r=16)
        ck_v = ck.rearrange("n (g r) d -> (n g) (r d)", r=8)
        cv_v = cv.rearrange("n (g r) d -> (n g) (r d)", r=8)

        def idma(out, in_, idx_ap, reads, writes):
            j = k.ndma
            k.ndma += 1
            i, v = j % k.ND, 16 * (j // k.ND + 1)
            if v > 16:
                k._wait("pool", ("d", k.dsem[i], v - 16))
            k._deps("pool", reads, writes)
            POOL.indirect_dma_start(out=out, out_offset=None, in_=in_,
                                    in_offset=bass.IndirectOffsetOnAxis(ap=idx_ap, axis=0)).then_inc(k.dsem[i], 16)
            k._commit(("d", k.dsem[i], v), reads, writes)

        def stageC_sample(l):
            key = ("win", l)
            hsrc = lambda kc: hT[:, kc, 0:P]
            SB = {n: B("s_" + n) for n in ("kpg", "kTs", "rl", "scs", "mks", "maskTs", "kpgK", "KTs", "vb", "mkp")}
            k.handoff([KTB, VRB, KIB, QTB], list(SB.values()))
            slotK, kk = wload(key, c.oK, 512)
            bi = mm_tm(slotK, kk, 0, 512, hsrc, [HTB])
            oB, ot = tmp_l(F32, 512)
            act(ot, psA[:, bi, :], AF.Copy, [pbA[bi]], [oB])
            k.dma("pool", k_s[l], ot, reads=[oB], writes=[OUTB()])
            ts(kn_tok[:], ot[:, 0:P], selt[:, 0:1], None, ALU.mult, None, [oB, B("selt")], [B("kn_tok")])
            for n in range(1, 4):
                stt(kn_tok[:], ot[:, n * P:(n + 1) * P], selt[:, n:n + 1], kn_tok[:], ALU.mult, ALU.add, [oB, B("selt"), B("kn_tok")], [B("kn_tok")])
            slotV, kk = wload(key, c.oVV, 512)
            bi = mm_tm(slotV, kk, 0, 512, hsrc, [HTB])
            oB, ot = tmp_l(F32, 512)
            act(ot, psA[:, bi, :], AF.Copy, [pbA[bi]], [oB])
            k.dma("pool", v_s[l], ot, reads=[oB], writes=[OUTB()])
            vfB, vf = tmp_s(F32, P)
            ts(vf, ot[:, 0:P], selt[:, 0:1], None, ALU.mult, None, [oB, B("selt")], [vfB])
            for n in range(1, 4):
                stt(vf, ot[:, n * P:(n + 1) * P], selt[:, n:n + 1], vf, ALU.mult, ALU.add, [oB, B("selt"), vfB], [vfB])
            k.op("dve", lambda: DVE.memset(vn1[:, 128:130], 1.0), writes=[B("vn1")])
            k.op("dve", lambda: DVE.tensor_copy(out=vn1[:, 0:P], in_=vf), reads=[vfB], writes=[B("vn1")])
            slotI, kk = wload(key, c.oKI, 80)
            bi = mm_tm(slotI, kk, 0, 80, hsrc, [HTB])
            kwB, kw = tmp_s(F32, 80)
            act(kw, psA[:, bi, 0:80], AF.Copy, [pbA[bi]], [kwB])
            k.dma("pool", ki_s[l], kw[:, 0:64], reads=[kwB], writes=[OUTB()])
            bi = mm_fm(slotI, kk, 0, 64, hsrc, [HTB], P)
            act(kiTs[:, :], psA[0:64, bi, 0:P], AF.Copy, [pbA[bi]], [B("kiTs")])
            bi = mm_fm(slotI, kk, 64, 16, hsrc, [HTB], P)
            act(wTf[:, :], psA[0:16, bi, 0:P], AF.Copy, [pbA[bi]], [B("wTf")])
            snB, sn16 = tmp_s(F32, 16)
            for q4 in range(2):
                slotQ, kk = wload(key, c.oQI + q4 * 512, 512)
                for j in range(8):
                    bi = mm_fm(slotQ, kk, j * 64, 64, hsrc, [HTB], P)
                    act(qiTs[:, q4 * 8 + j, :], psA[0:64, bi, 0:P], AF.Copy, [pbA[bi]], [X2b])
                bi = mm_tm(slotQ, kk, 0, 512, hsrc, [HTB])
                pB2, prod = tmp_l(F32, 512)
                tt("dve", prod.rearrange("p (h d) -> p h d", d=64), psA[:, bi, :].rearrange("p (h d) -> p h d", d=64),
                   kw[:, 0:64].unsqueeze(1).to_broadcast([P, 8, 64]), ALU.mult, [pbA[bi], kwB], [pB2])
                k.op("dve", lambda: DVE.reduce_sum(out=sn16[:, q4 * 8:(q4 + 1) * 8], in_=prod.rearrange("p (h d) -> p h d", d=64), axis=AX.X),
                     reads=[pB2], writes=[snB])
            ts(sn16, sn16, 0.0, None, ALU.max, None, [snB], [snB])
            tt("dve", sn16, sn16, kw[:, 64:80], ALU.mult, [snB, kwB], [snB])
            k.op("dve", lambda: DVE.reduce_sum(out=scs[:, PAST:PAST + 1], in_=sn16, axis=AX.X), reads=[snB], writes=[SB["scs"]])
            qfB, qTf = tmp_l(F32, 512)
            qTf3 = qTf.rearrange("p (g q) -> p g q", q=P)
            qnf = qn_tok[:, :, :].rearrange("p g d -> p (g d)")
            for n in range(4):
                slotQ, kk = wload(key, c.oQ + n * 512, 512)
                bi = mm_tm(slotQ, kk, 0, 512, hsrc, [HTB])
                if n == 0:
                    ts(qnf, psA[:, bi, :], selt[:, 0:1], None, ALU.mult, None, [pbA[bi], B("selt")], [B("qn_tok")])
                else:
                    stt(qnf, psA[:, bi, :], selt[:, n:n + 1], qnf, ALU.mult, ALU.add, [pbA[bi], B("selt"), B("qn_tok")], [B("qn_tok")])
                for g in range(4):
                    bi = mm_fm(slotQ, kk, g * P, P, hsrc, [HTB], P)
                    if n == 0:
                        ts(qTf3[:, g, :], psA[:, bi, 0:P], selt[:, 0:1], None, ALU.mult, None, [pbA[bi], B("selt")], [qfB])
                    else:
                        stt(qTf3[:, g, :], psA[:, bi, 0:P], selt[:, n:n + 1], qTf3[:, g, :], ALU.mult, ALU.add, [pbA[bi], B("selt"), qfB], [qfB])
            k.op("dve", lambda: DVE.tensor_copy(out=qTn[:, :, :], in_=qTf3), reads=[qfB], writes=[B("qTn")])
            nq = (PAST + 511) // 512
            for b in range(NS):
                idma(kgi[0:MI, :, :].rearrange("p r d -> p (r d)"), cki_v, idxI[0:MI, b:b + 1], [IXB], [SB["kpg"]])
                bt4 = [bank() for _ in range(nq)]

                def f():
                    for r in range(16):
                        cc = r * MI
                        ins = PE.transpose(out=psA[0:64, bt4[cc // 512], cc % 512:cc % 512 + MI], in_=kgi[0:MI, r, :], identity=identf[0:MI, 0:MI])
                    return ins
                k.op("pe", f, reads=[SB["kpg"], B("identf")], writes=[pbA[x] for x in bt4])
                for q in range(nq):
                    wdt = min(512, PAST - q * 512)
                    act(kTs[0:64, q * 512:q * 512 + wdt], psA[0:64, bt4[q], 0:wdt], AF.Copy, [pbA[bt4[q]]], [SB["kTs"]])
                stt(Wsel[:, b % 2, :], iota_f[:], float(b), wTf[:, :], ALU.is_equal, ALU.mult, [B("iota"), B("wTf")], [B("Wsel")])
                for q in range(nq):
                    k0 = q * 512
                    wdt = min(512, PAST - k0)
                    bi = bank()
                    k.op("pe", lambda: PE.matmul(out=psA[0:16, bi, 0:wdt], lhsT=qiTs[:, :, b], rhs=kTs[0:64, k0:k0 + wdt], start=True, stop=True),
                         reads=[X2b, SB["kTs"]], writes=[pbA[bi]])
                    act(rl[0:16, k0:k0 + wdt], psA[0:16, bi, 0:wdt], AF.Relu, [pbA[bi]], [SB["rl"]])
                    k.op("pe", lambda: PE.matmul(out=psS[q][:, 0:wdt], lhsT=Wsel[:, b % 2, :], rhs=rl[0:16, k0:k0 + wdt],
                                                 start=(b == 0), stop=(b == NS - 1)),
                         reads=[B("Wsel"), SB["rl"]], writes=[psSB[q]])
            for q in range(nq):
                k0 = q * 512
                wdt = min(512, PAST - k0)
                act(scs[:, k0:k0 + wdt], psS[q][:, 0:wdt], AF.Copy, [psSB[q]], [SB["scs"]])
            for it in range(c.TOPK_S // 8):
                k.op("dve", lambda: DVE.max(out=m8[:], in_=scs[:, 0:PK]), reads=[SB["scs"]], writes=[B("m8")])
                k.op("dve", lambda: DVE.match_replace(out=scs[:, 0:PK], in_to_replace=m8[:], in_values=scs[:, 0:PK], imm_value=NEG_TOPK),
                     reads=[SB["scs"], B("m8")], writes=[SB["scs"]])
            ts(mks[:, 0:PK], scs[:, 0:PK], -2.0e30, None, ALU.is_le, None, [SB["scs"]], [SB["mks"]])
            mk6 = mks[:, 0:PAST].rearrange("p (v r h j u) -> p v r h j u", v=2, r=8, h=NHK, j=JP, u=8)

            for h in range(NHK):
                for r in range(8):
                    bk = h * 8 + r
                    k.op("dve", lambda: DVE.tensor_copy(out=mkp[:, bk, :].rearrange("p (j u v) -> p j u v", u=8, v=2),
                                                        in_=mk6[:, :, r, h, :, :].rearrange("p v j u -> p j u v")),
                         reads=[SB["mks"]], writes=[SB["mkp"]])
            transposes_bf(lambda bk: mkp[:, bk, :], NB, [SB["mkp"]])
            act(maskTs[0:MK, :, :], psT[0:MK, 0:NB * P].rearrange("p (j q) -> p j q", q=P), AF.Copy, [B("psT")], [SB["maskTs"]])
            pB3, prod = tmp_l(F32, 512)
            tt("dve", prod.rearrange("p (g d) -> p g d", d=P), qn_tok[:, :, :], kn_tok[:, :].unsqueeze(1).to_broadcast([P, 4, P]), ALU.mult,
               [B("qn_tok"), B("kn_tok")], [pB3])
            pnB, pnew = tmp_s(F32, 4)
            k.op("dve", lambda: DVE.reduce_sum(out=pnew, in_=prod.rearrange("p (g d) -> p g d", d=P), axis=AX.X), reads=[pB3], writes=[pnB])
            act(pnew, pnew, AF.Exp, [pnB], [pnB], scale=float(c.HD) ** -0.5)
            mfB, mf = tmp_s(F32, 1)
            k.op("dve", lambda: DVE.tensor_copy(out=mf, in_=mks[:, PAST:PAST + 1]), reads=[SB["mks"]], writes=[mfB])
            ts(pnew, pnew, mf, None, ALU.mult, None, [pnB, mfB], [pnB])
            tt("dve", Pn[:, :, :], identf[:, :].unsqueeze(2).to_broadcast([P, P, 4]), pnew.unsqueeze(1).to_broadcast([P, P, 4]), ALU.mult,
               [B("identf"), pnB], [B("Pn")])
            k.handoff([SB["kpg"], SB["kTs"], SB["rl"], SB["scs"]], [SB["kpgK"], SB["KTs"]])
            k.op("dve", lambda: DVE.memset(vb[:, :, 128:130], 1.0), writes=[SB["vb"]])
            ocB = []
            for b in range(NS):
                for h in range(NHK):
                    idma(kgK[0:MK, h, :, :].rearrange("p r d -> p (r d)"), ck_v, idxK[0:MK, h, b:b + 1], [IXB], [SB["kpgK"]])
                for h in range(NHK):
                    idma(vgV[0:MK, h, :, :].rearrange("p r d -> p (r d)"), cv_v, idxK[0:MK, h, b:b + 1], [IXB], [VNB])
                bt4 = [bank() for _ in range(nq)]

                def f():
                    for bk in range(NB):
                        cc = bk * MK
                        ins = PE.transpose(out=psA[:, bt4[cc // 512], cc % 512:cc % 512 + MK], in_=kgK[0:MK, bk // 8, bk % 8, :],
                                           identity=identf[0:MK, 0:MK])
                    return ins
                k.op("pe", f, reads=[SB["kpgK"], B("identf")], writes=[pbA[x] for x in bt4])
                for q in range(nq):
                    wdt = min(512, PAST - q * 512)
                    act(KTs[:, q * 512:q * 512 + wdt], psA[:, bt4[q], 0:wdt], AF.Copy, [pbA[bt4[q]]], [SB["KTs"]])
                k.op("pool", lambda: POOL.tensor_copy(out=vb[0:MK, :, 0:P], in_=vgV[0:MK, :, :, :].rearrange("p h r d -> p (h r) d")),
                     reads=[VNB], writes=[SB["vb"]])
                bS = bank()

                def f2():
                    for bk in range(NB):
                        ins = PE.matmul(out=psA[0:MK, bS, bk * 4:(bk + 1) * 4], lhsT=KTs[:, bk * MK:(bk + 1) * MK], rhs=qTn[:, :, b], start=True, stop=True)
                    return ins
                k.op("pe", f2, reads=[SB["KTs"], B("qTn")], writes=[pbA[bS]])
                eB, e = tmp_s(BF16, NB * 4)
                act(e[0:MK, :], psA[0:MK, bS, 0:NB * 4], AF.Exp, [pbA[bS]], [eB], scale=float(c.HD) ** -0.5)
                ptB, pT = tmp_s(BF16, NB * 4)
                tt("dve", pT[0:MK, :].rearrange("p (j g) -> p j g", g=4), e[0:MK, :].rearrange("p (j g) -> p j g", g=4),
                   maskTs[0:MK, :, b].unsqueeze(2).to_broadcast([MK, NB, 4]), ALU.mult, [eB, SB["maskTs"]], [ptB])
                bo = bank()

                def f3():
                    for bk in range(NB):
                        PE.matmul(out=psA[0:4, bo, 0:129], lhsT=pT[0:MK, bk * 4:(bk + 1) * 4], rhs=vb[0:MK, bk, 0:129], start=(bk == 0), stop=False)
                    return PE.matmul(out=psA[0:4, bo, 0:129], lhsT=Pn[:, b, :], rhs=vn1[:, 0:129], start=False, stop=True)
                k.op("pe", f3, reads=[ptB, SB["vb"], B("Pn"), B("vn1")], writes=[pbA[bo]])
                rvB, rv = tmp_s(F32, 1)
                k.op("dve", lambda: DVE.reciprocal(out=rv[0:4, :], in_=psA[0:4, bo, 128:129]), reads=[pbA[bo]], writes=[rvB])
                osB = B("osm%d" % (b % 2))
                ts(osm[0:4, b % 2, :], psA[0:4, bo, 0:P], rv[0:4, 0:1], None, ALU.mult, None, [pbA[bo], rvB], [osB])
                ob = Buf("ocb")
                ocB.append(ob)
                k.dma("pool", oc_b[l].ap()[b, :].rearrange("(g d) -> g d", d=P), osm[0:4, b % 2, :], reads=[osB], writes=[ob])
            k.collective("AllGather", oc_b[l].ap().opt(), oc_g[l].ap().opt(), reads=ocB, writes=[B("ocg%d" % l)])
            octok = X1t[:, 0:2048]
            for n in range(4):
                rank = 4 * l + n
                k.dma("sp", octok[:, n * 512:(n + 1) * 512], oc_g[l].ap()[rank * NS:(rank + 1) * NS, :], reads=[B("ocg%d" % l)], writes=[X1b])
            if KDBG and l == 0:
                for i_ in range(8):
                    dbg(12 + i_, octok[:, i_ * 256:(i_ + 1) * 256], X1b)
                for i_ in range(PAST // 256):
                    mfB_, mf_ = tmp_s(F32, 256)
                    k.op("dve", lambda: DVE.tensor_copy(out=mf_, in_=mks[:, i_ * 256:(i_ + 1) * 256]), reads=[SB["mks"]], writes=[mfB_])
                    dbg(20 + i_, mf_, mfB_)
            for n in range(4):
                slotZ, kk = wload(key, c.oZC + n * 512, 512)
                bi = mm_tm(slotZ, kk, 0, 512, hsrc, [HTB])
                szB, sz = tmp_l(F32, 512)
                act(sz, psA[:, bi, :], AF.Silu, [pbA[bi]], [szB])
                tt("dve", X3t[:, n * 512:(n + 1) * 512], octok[:, n * 512:(n + 1) * 512], sz, ALU.mult, [X1b, szB], [X3b])
            transposes_bf(lambda j: X3t[:, j * P:(j + 1) * P], 16, [X3b])
            act(brT[:, 0:16, 0:P], psT[:, 0:16 * P].rearrange("p (kk t) -> p kk t", t=P), AF.Copy, [B("psT")], [BRB])
            k.handoff(list(SB.values()), [KTB, VRB, KIB, QTB])

        _stop = int(os.environ.get("KSTOP", "100000"))
        _cnt = [0]

        def go():
            _cnt[0] += 1
            return _cnt[0] <= _stop

        for l in range(2):
            if go(): layer_setup(l)
            if go(): mod_stage(l)
            xin_p = xp if l == 0 else x1p.ap()
            xout_p = x1p.ap() if l == 0 else yp
            xin_s = xs if l == 0 else x1s.ap()
            xout_s = x1s.ap() if l == 0 else ys
            for ti in range(NTT):
                rows = lambda s, ti=ti: slice((ti * NSUB + s) * P, (ti * NSUB + s + 1) * P)
                inB = lambda s, ti=ti, l=l: B("xin%d_%d" % (l, ti * NSUB + s))
                outB = lambda s, ti=ti, l=l: B("xin%d_%d" % (l + 1, ti * NSUB + s))
                for s in range(NSUB):
                    if go(): pre(l, 0, xin_p[rows(s), :], inB(s), s)
                if go(): stageA(l, 0, TT)
                if go(): proj(l, 0, "wpa", TT)
                if go(): stageB(l, 0, TT, ti == NTT - 1)
                if go(): proj(l, 1, "wpb", TT)
                if go(): stageC_prompt(l, ti)
                if go(): proj(l, 2, "wpc", TT)
                if go(): out_post(l, 0, TT, lambda s: xin_p[rows(s), :], inB, lambda s: xout_p[rows(s), :], outB)
            sinB = lambda s, l=l: B("xsin%d" % l)
            soutB = lambda s, l=l: B("xsin%d" % (l + 1))
            if go(): pre(l, 1, xin_s, sinB(0), 0)
            if go(): stageA(l, 1, P)
            if go(): proj(l, 0, "wpa", P)
            if go(): stageB(l, 1, P, False)
            if go(): proj(l, 1, "wpb", P)
            if go(): stageC_sample(l)
            if go(): proj(l, 2, "wpc", P)
            if go(): out_post(l, 1, P, lambda s: xin_s, sinB, lambda s: xout_s, soutB)
        k.finish()
        if os.environ.get("KPRINT"):
            print("KSTATS ndma", k.ndma, "cnt", k.cnt, "ncc", k.ncc)
    return nc


_NC_CACHE = {}


def run(cfg, inp):
    c = cfg
    key = (c.D, c.T, c.PAST)
    if key not in _NC_CACHE:
        _NC_CACHE[key] = build(c)
    nc = _NC_CACHE[key]
    f32 = np.float32
    A = lambda a: np.ascontiguousarray(np.asarray(a))
    D = c.D
    R8, RB8 = D // NCORES, c.WA // NCORES
    in_maps = []
    for core in range(NCORES):
        b, l, n = core % 4, core // 4, core % 4
        sel = np.zeros((P, 4), f32)
        sel[:, n] = 1.0
        m = {
            "xp": A(inp["x_prompt"][b]), "xs": A(inp["x_sample"][:, 0, :]),
            "cp": A(np.broadcast_to(np.asarray(inp["c_prompt"])[b][None, :], (P, D))), "cs": A(inp["c_sample"]),
            "ck": A(np.asarray(inp["cache_k"])[l, :, :, n, :]), "cv": A(np.asarray(inp["cache_v"])[l, :, :, n, :]),
            "cki": A(np.asarray(inp["cache_kidx"])[l]),
            "sconv": A(inp["state_conv"]), "sh": A(inp["state_h"]), "pt": A(inp["page_table"]).astype(np.int32),
            "sel": sel,
        }
        for nm, src_name in (("wmod", "w_mod"), ("win", "w_in"), ("wpa", "w_pa"), ("wpb", "w_pb"), ("wpc", "w_pc"), ("wout", "w_out")):
            wfull = np.asarray(inp[src_name])
            shard = np.zeros((2, wfull.shape[1], c.NBR[nm] * 512), f32)
            for bi_, (c0, ncols) in enumerate(c.BL[nm]):
                if bi_ % NCORES == core:
                    s_ = bi_ // NCORES
                    shard[:, :, s_ * 512:s_ * 512 + ncols] = wfull[:, :, c0:c0 + ncols]
            m[nm + "_s"] = shard
        for nm in ("b_mod", "g_pre", "g_v", "w_s", "b_s", "w_conv", "b_conv", "w_rg_a", "b_rg_a", "w_rg_x", "b_rg_x", "lam", "g_post"):
            m[nm] = A(inp[nm])
        in_maps.append(m)
    res = run_bass_kernel_spmd(nc, in_maps, core_ids=list(range(NCORES))).results
    R = lambda core, name: np.asarray(res[core][name])
    if os.environ.get("KDBG"):
        np.save("dbg.npy", R(0, "dbg"))
    T, NS, WB, WA = c.T, c.NS, c.WB, c.WA
    y_p = np.stack([R(b, "yp") for b in range(4)])
    y_s = R(0, "ys")[:, None, :]
    k_p = np.stack([R(b, "k_p") for b in range(4)], axis=1).reshape(2, 4, T, 4, 128)
    v_p = np.stack([R(b, "v_p") for b in range(4)], axis=1).reshape(2, 4, T, 4, 128)
    ki_p = np.stack([R(b, "ki_p") for b in range(4)], axis=1)
    cv_p = np.stack([R(b, "conv_p") for b in range(4)], axis=1)
    h_p = np.stack([R(b, "h_p") for b in range(4)], axis=1)
    k_s = R(0, "k_s").reshape(2, NS, 1, 4, 128)
    v_s = R(0, "v_s").reshape(2, NS, 1, 4, 128)
    ki_s = R(0, "ki_s").reshape(2, NS, 1, c.ID)
    cv_s = R(0, "conv_s")
    h_s = R(0, "h_s")
    gv_s = R(0, "gv_s").reshape(2, NS, 1, WA)
    outs = (y_p, y_s, k_p, v_p, ki_p, cv_p, h_p, k_s, v_s, ki_s, cv_s, h_s, gv_s)
    return tuple(np.ascontiguousarray(o, dtype=np.float32) for o in outs)


def kernel(**inputs):
    return run(Cfg(), inputs)
```
